# Optimizing a Trainium2 kernel written in Bass

```python
import math
import jax, jax.numpy as jnp
from jax import lax
import numpy as np

D_MODEL = 1024
BATCH = 2
SEQ = 8192
DEPTH = 1

N_HEADS = 8
HEAD_DIM = 128
ATTN_WIDTH = N_HEADS * HEAD_DIM
MOBA_BLOCK = 256
MOBA_TOPK = 3
Q_CHUNK = 64
ROPE_THETA = 500000.0
ROPE_DIM = HEAD_DIM // 4
CONV_WIDTH = D_MODEL
CONV_K = 3
D_FF = ((8 * D_MODEL // 3 + 255) // 256) * 256
EPS = 1e-6
NEG = -1e30

SPLIT_WIDTHS = [ATTN_WIDTH, ATTN_WIDTH, ATTN_WIDTH,
                CONV_WIDTH, CONV_WIDTH, CONV_WIDTH,
                D_MODEL, D_MODEL]
IN_WIDTH = int(sum(SPLIT_WIDTHS))
SPLIT_IDX = [int(i) for i in np.cumsum(SPLIT_WIDTHS)[:-1]]

kernel_name = "hybrid_moba_shortconv_gated_block"


def rmsnorm(x, g):
    xf = x.astype(jnp.float32)
    y = xf * lax.rsqrt(jnp.mean(xf * xf, axis=-1, keepdims=True) + EPS)
    return (y * g.astype(jnp.float32)).astype(x.dtype)


def partial_rope(t, positions):
    rot, rest = t[..., :ROPE_DIM], t[..., ROPE_DIM:]
    inv_freq = ROPE_THETA ** (-jnp.arange(0, ROPE_DIM, 2, dtype=jnp.float32) / ROPE_DIM)
    ang = positions.astype(jnp.float32)[:, None] * inv_freq[None, :]
    cos, sin = jnp.cos(ang), jnp.sin(ang)
    r = rot.astype(jnp.float32)
    r1, r2 = r[..., : ROPE_DIM // 2], r[..., ROPE_DIM // 2:]
    out = jnp.concatenate([r1 * cos - r2 * sin, r2 * cos + r1 * sin], axis=-1)
    return jnp.concatenate([out.astype(t.dtype), rest], axis=-1)


def moba_attention(q, k, v):
    B, H, S, Dh = q.shape
    S_pad = -(-S // MOBA_BLOCK) * MOBA_BLOCK
    pad = [(0, 0), (0, 0), (0, S_pad - S), (0, 0)]
    q_p, k_p, v_p = jnp.pad(q, pad), jnp.pad(k, pad), jnp.pad(v, pad)
    NB = S_pad // MOBA_BLOCK
    NC = S_pad // Q_CHUNK
    topk = min(MOBA_TOPK, NB)
    scale = 1.0 / math.sqrt(Dh)
    kb = k_p.reshape(B, H, NB, MOBA_BLOCK, Dh)
    vb = v_p.reshape(B, H, NB, MOBA_BLOCK, Dh)
    kmean = kb.astype(jnp.float32).mean(axis=3)
    bi = jnp.arange(B)[:, None, None, None]
    hi = jnp.arange(H)[None, :, None, None]
    q_chunks = q_p.reshape(B, H, NC, Q_CHUNK, Dh).transpose(2, 0, 1, 3, 4)

    def step(args):
        c, qc = args
        q0 = c * Q_CHUNK
        blk = q0 // MOBA_BLOCK
        gate = jnp.einsum('bhqd,bhnd->bhqn', qc.astype(jnp.float32), kmean)
        gate = jnp.where(jnp.arange(NB) < blk, gate, NEG)
        _, idx = lax.top_k(gate, topk)
        valid = jnp.arange(topk) < blk
        kg = kb[bi, hi, idx]
        vg = vb[bi, hi, idx]
        s_past = jnp.einsum('bhqd,bhqjkd->bhqjk', qc, kg).astype(jnp.float32) * scale
        s_past = jnp.where(valid[:, None], s_past, NEG)
        k_own = lax.dynamic_slice_in_dim(k_p, blk * MOBA_BLOCK, MOBA_BLOCK, axis=2)
        v_own = lax.dynamic_slice_in_dim(v_p, blk * MOBA_BLOCK, MOBA_BLOCK, axis=2)
        s_own = jnp.einsum('bhqd,bhkd->bhqk', qc, k_own).astype(jnp.float32) * scale
        causal = (q0 + jnp.arange(Q_CHUNK))[:, None] >= (blk * MOBA_BLOCK + jnp.arange(MOBA_BLOCK))[None, :]
        s_own = jnp.where(causal, s_own, NEG)
        logits = jnp.concatenate([s_past.reshape(B, H, Q_CHUNK, topk * MOBA_BLOCK), s_own], axis=-1)
        p = jax.nn.softmax(logits, axis=-1).astype(v.dtype)
        p_past = p[..., : topk * MOBA_BLOCK].reshape(B, H, Q_CHUNK, topk, MOBA_BLOCK)
        p_own = p[..., topk * MOBA_BLOCK:]
        return (jnp.einsum('bhqjk,bhqjkd->bhqd', p_past, vg)
                + jnp.einsum('bhqk,bhkd->bhqd', p_own, v_own))

    out = lax.map(step, (jnp.arange(NC, dtype=jnp.int32), q_chunks))
    out = out.transpose(1, 2, 0, 3, 4).reshape(B, H, S_pad, Dh)
    return out[:, :, :S]


def causal_depthwise_conv(z, w):
    C = z.shape[-1]
    return lax.conv_general_dilated(
        z, w[:, None, :].astype(z.dtype), window_strides=(1,),
        padding=[(CONV_K - 1, 0)], dimension_numbers=('NWC', 'WIO', 'NWC'),
        feature_group_count=C)


def setup_inputs(seed: int = 0) -> dict:
    key = jax.random.key(seed)
    ks = jax.random.split(key, 12)
    f32 = jnp.float32
    def nrm(k, shape, fan_in):
        return jax.random.normal(k, shape, f32) * (fan_in ** -0.5)
    def gain(k, n):
        return 1.0 + 0.05 * jax.random.normal(k, (DEPTH, n), f32)
    return {
        "x": jax.random.normal(ks[0], (BATCH, SEQ, D_MODEL), f32),
        "g_mix": gain(ks[1], D_MODEL),
        "w_in": nrm(ks[2], (DEPTH, D_MODEL, IN_WIDTH), D_MODEL),
        "conv_w": nrm(ks[3], (DEPTH, CONV_K, CONV_WIDTH), CONV_K),
        "w_attn_branch": nrm(ks[4], (DEPTH, ATTN_WIDTH, D_MODEL), ATTN_WIDTH),
        "w_conv_branch": nrm(ks[5], (DEPTH, CONV_WIDTH, D_MODEL), CONV_WIDTH),
        "w_out": nrm(ks[6], (DEPTH, D_MODEL, D_MODEL), D_MODEL),
        "g_ffn": gain(ks[7], D_MODEL),
        "w_gate_up": nrm(ks[8], (DEPTH, D_MODEL, 2 * D_FF), D_MODEL),
        "w_down": nrm(ks[9], (DEPTH, D_FF, D_MODEL), D_FF),
        "g_final": 1.0 + 0.05 * jax.random.normal(ks[10], (D_MODEL,), f32),
    }


def reference(x, g_mix, w_in, conv_w, w_attn_branch, w_conv_branch, w_out,
              g_ffn, w_gate_up, w_down, g_final):
    B, S, _ = x.shape
    positions = jnp.arange(S, dtype=jnp.int32)
    h = x
    for l in range(DEPTH):
        u = rmsnorm(h, g_mix[l])
        proj = u @ w_in[l]
        q, k, v, c_gate, b_gate, xc, ga, gc = jnp.split(proj, SPLIT_IDX, axis=-1)
        to_heads = lambda t: t.reshape(B, S, N_HEADS, HEAD_DIM).transpose(0, 2, 1, 3)
        qh = partial_rope(to_heads(q), positions)
        kh = partial_rope(to_heads(k), positions)
        vh = to_heads(v)
        attn = moba_attention(qh, kh, vh).transpose(0, 2, 1, 3).reshape(B, S, ATTN_WIDTH)
        y_attn = attn @ w_attn_branch[l]
        zc = causal_depthwise_conv(c_gate * xc, conv_w[l])
        y_conv = (b_gate * zc) @ w_conv_branch[l]
        merged = jax.nn.sigmoid(ga) * y_attn + jax.nn.sigmoid(gc) * y_conv
        h = h + merged @ w_out[l]
        u2 = rmsnorm(h, g_ffn[l])
        gate, up = jnp.split(u2 @ w_gate_up[l], 2, axis=-1)
        h = h + (jax.nn.silu(gate) * up) @ w_down[l]
    return rmsnorm(h, g_final)
```

```python
from contextlib import ExitStack
import os
import numpy as np
import concourse.bass as bass
import concourse.mybir as mybir
from concourse.bass_utils import run_bass_kernel_spmd

F32 = mybir.dt.float32
BF16 = mybir.dt.bfloat16
AF = mybir.ActivationFunctionType
ALU = mybir.AluOpType
AX = mybir.AxisListType

D = 1024
S = 8192
H = 8
DH = 128
NCH = 16
NBLK = 32
DFF = 2816
EPS = 1e-6
NEG = -1e30
BIG = 30000.0
SCALE = 1.0 / float(np.sqrt(DH))
ROPE_THETA = 500000.0
VW = 132

ENGINES = ("sync", "scalar", "vector", "gpsimd", "tensor")
STRICT_SAME_ENGINE = True


class R:
    __slots__ = ("name", "w", "r")

    def __init__(self, name):
        self.name = name
        self.w = None
        self.r = {}


class Prog:
    def __init__(self, nc):
        self.nc = nc
        self.ops = {e: [] for e in ENGINES}
        self.semh = {}
        self.count = {}
        self.waited = {e: {} for e in ENGINES}
        self.stack = ExitStack()

    def sem(self, key):
        if key not in self.semh:
            self.semh[key] = self.stack.enter_context(self.nc.semaphore("s%d" % len(self.semh)))
            self.count[key] = 0
        return self.semh[key]

    def _need(self, eng, waits, tok, kind):
        if tok is None:
            return
        key, val, peng, is_dma = tok
        if (not is_dma) and peng == eng and (eng == "tensor" or (kind != "RAW" and not STRICT_SAME_ENGINE)):
            return
        if self.waited[eng].get(key, 0) >= val:
            return
        if waits.get(key, 0) < val:
            waits[key] = val

    def op(self, eng, fn, reads=(), writes=(), dma=0, semkey=None):
        waits = {}
        for r in reads:
            self._need(eng, waits, r.w, "RAW")
        for w in writes:
            self._need(eng, waits, w.w, "WAW")
            for tok in w.r.values():
                self._need(eng, waits, tok, "WAR")
        for k, v in waits.items():
            self.waited[eng][k] = v
        if dma:
            key = semkey if semkey is not None else ("d", writes[0].name)
            self.sem(key)
            self.count[key] += 16 * dma
            tok = (key, self.count[key], eng, True)
        else:
            key = ("e", eng)
            self.sem(key)
            self.count[key] += 1
            tok = (key, self.count[key], eng, False)
        for w in writes:
            w.w = tok
            w.r = {}
        for r in reads:
            r.r[key] = tok
        self.ops[eng].append((list(waits.items()), fn, key, bool(dma), dma))
        return tok

    def wait_all(self, eng, toks):
        waits = {}
        for tok in toks:
            key, val, _, _ = tok
            if waits.get(key, 0) < val:
                waits[key] = val
        self.ops[eng].append((list(waits.items()), None, None, False, 0))

    def barrier(self):
        for eng in ENGINES:
            w = [(k, v) for k, v in self.count.items() if v > 0 and self.waited[eng].get(k, 0) < v]
            for k, v in w:
                self.waited[eng][k] = v
            self.ops[eng].append((w, None, None, False, 0))

    def emit(self):
        nc = self.nc
        with nc.Block() as block:
            for eng in ENGINES:
                ops = self.ops[eng]
                if not ops:
                    continue

                def body(e, ops=ops):
                    for waits, fn, key, is_dma, ndma in ops:
                        for k, v in waits:
                            e.wait_ge(self.semh[k], v)
                        if fn is None:
                            continue
                        ins = fn(e)
                        if is_dma:
                            if not isinstance(ins, (list, tuple)):
                                ins = [ins]
                            assert len(ins) == ndma, (len(ins), ndma)
                            for i in ins:
                                i.then_inc(self.semh[key], 16)
                        else:
                            if isinstance(ins, (list, tuple)):
                                ins = ins[-1]
                            ins.then_inc(self.semh[key], 1)

                getattr(block, eng)(body)


class Buf:
    def __init__(self, t, name, n=1):
        self.t = t
        self.R = R(name)
        self.Rs = [R("%s_%d" % (name, i)) for i in range(n)]


def build(stage="full"):
    nc = bass.Bass("TRN2", target_bir_lowering=False)
    dt = lambda name, shape, dtype=F32, kind="ExternalInput": nc.dram_tensor(name, shape, dtype, kind=kind)
    xb = dt("xb", [S, D])
    tabK = dt("tabK", [128, 64, 64])
    wkv = dt("wkv", [D, 2 * D])
    wq = dt("wq", [D, D])
    gtab = dt("gtab", [128, 40])
    xh = dt("xh", [8, D])
    gfin = dt("gfin", [128, D])
    wrest = dt("wrest", [D, 5 * D])
    w_ab = dt("w_ab", [D, D]); w_cb = dt("w_cb", [D, D]); w_o = dt("w_o", [D, D])
    w_gu = dt("w_gu", [D, 2 * DFF]); w_dn = dt("w_dn", [DFF, D])
    mtab = dt("mtab", [128, 3, 16, 32])
    cst = dt("cst", [128, 256])
    eoh = dt("eoh", [128, 32, 128])
    if stage == "A":
        kt_scr = dt("kt_scr", [128, H, S], BF16, "ExternalOutput")
        v_scr = dt("v_scr", [128, H, 64, VW], BF16, "ExternalOutput")
        o_qt = dt("o_qt", [128, H, 2048], BF16, "ExternalOutput")
        o_km = dt("o_km", [128, H, NBLK], F32, "ExternalOutput")
    else:
        if stage == "full":
            out_d = dt("out", [2048, D], F32, "ExternalOutput")
        if stage == "B":
            o_at = dt("o_at", [128, H, 2048], BF16, "ExternalOutput")
            o_qt2 = dt("o_qt2", [128, H, 2048], BF16, "ExternalOutput")
            o_d1 = dt("o_d1", [128, 4, 32], F32, "ExternalOutput")
            o_d2 = dt("o_d2", [128, 4, 32], BF16, "ExternalOutput")
            o_d3 = dt("o_d3", [32, 512], BF16, "ExternalOutput")
        kt_scr = nc.dram_tensor("kt_scr", [128, H, S], BF16)
        v_scr = nc.dram_tensor("v_scr", [128, H, 64, VW], BF16)

    P = Prog(nc)
    outs = []
    with P.stack:
        st = P.stack

        def sb(name, shape, dtype, n=1, stack=st):
            return Buf(stack.enter_context(nc.sbuf_tensor(name, shape, dtype)), name, n)

        def ps(name, shape, dtype, n=1, stack=st):
            return Buf(stack.enter_context(nc.psum_tensor(name, shape, dtype)), name, n)

        ident = sb("ident", [128, 128], BF16)
        identf = sb("identf", [128, 128], F32)
        gt = sb("gt", [128, 40], F32)
        kmeanT = sb("kmeanT", [128, H, NBLK], BF16)
        kmeanF = sb("kmeanF", [128, H, 2 * NBLK], F32)
        QTW = sb("QTW", [128, 22 * D], BF16, n=4)
        UT = sb("UT", [128, 8 * 2056 + 8 * 2048], BF16, n=4)
        class _V:
            pass
        QT = _V(); QT.t = QTW.t[:, 0:H * 2048].rearrange("p (h t) -> p h t", h=H); QT.Rs = QTW.Rs; QT.R = QTW.R
        uTo = _V(); uTo.t = UT.t[:, 0:8 * 2056].rearrange("p (k t) -> p k t", k=8); uTo.Rs = [R("uTo%d" % q) for q in range(5)]; uTo.R = R("uTo")
        attnT = _V(); attnT.t = UT.t[:, 8 * 2056:8 * 2056 + 8 * 2048].rearrange("p (h t) -> p h t", h=H); attnT.Rs = [R("attnT%d" % q) for q in range(4)]

        P.op("gpsimd", lambda e: e.memset(identf.t[:], 1.0), writes=[identf.R])
        P.op("gpsimd", lambda e: e.affine_select(out=identf.t[:], in_=identf.t[:], pattern=[[-1, 128]],
                                                 compare_op=ALU.is_equal, fill=0.0, base=0, channel_multiplier=1),
             reads=[identf.R], writes=[identf.R])
        P.op("vector", lambda e: e.tensor_copy(out=ident.t[:], in_=identf.t[:]), reads=[identf.R], writes=[ident.R])
        P.op("sync", lambda e: e.dma_start(out=gt.t[:], in_=gtab[:, :]), writes=[gt.R], dma=1)

        with ExitStack() as sa:
            wkv_sb = sb("wkv_sb", [128, 8, 2 * D], BF16, stack=sa)
            wq_sb = _V(); wq_sb.t = UT.t[:, 8 * 2056:8 * 2056 + 8 * D].rearrange("p (k c) -> p k c", k=8); wq_sb.R = R("wq_sb")
            NX = 2
            xt = [sb("xt%d" % i, [128, D], F32, stack=sa) for i in range(NX)]
            ss = [sb("ss%d" % i, [128, 2], F32, stack=sa) for i in range(2)]
            xn = [sb("xn%d" % i, [128, D], BF16, stack=sa) for i in range(2)]
            uT = [sb("uT%d" % i, [128, 8, 512], BF16, n=4, stack=sa) for i in range(2)]
            tb = [sb("tb%d" % i, [128, 4, 64], F32, stack=sa) for i in range(2)]
            ktok = [sb("ktok%d" % i, [128, H, 128], BF16, n=4, stack=sa) for i in range(2)]
            rtmp = [sb("rtmp%d" % i, [128, H, 64], F32, n=6, stack=sa) for i in range(2)]
            KTc = [sb("KTc0", [128, H, 512], BF16, n=4, stack=sa)] * 2
            Vc = [sb("Vc%d" % i, [128, H, 4, VW], BF16, n=2, stack=sa) for i in range(2)]
            kps = ps("kps", [128, 2 * 512], F32, n=2, stack=sa)
            vps = ps("vps", [128, 2 * 512], F32, n=2, stack=sa)
            ktp = [ps("ktp%d" % i, [128, H, 128], BF16, stack=sa) for i in range(2)]

            for kt in range(8):
                P.op("gpsimd", lambda e, kt=kt: e.dma_start(out=wkv_sb.t[:, kt, :], in_=wkv[kt * 128:(kt + 1) * 128, :]),
                     writes=[wkv_sb.R], dma=1)
            for kt in range(8):
                P.op("gpsimd", lambda e, kt=kt: e.dma_start(out=wq_sb.t[:, kt, :], in_=wq[kt * 128:(kt + 1) * 128, :]),
                     writes=[wq_sb.R], dma=1)
            for v in Vc:
                P.op("gpsimd", lambda e, v=v: e.memset(v.t[:, :, :, 128:VW], 1.0), writes=[v.R])

            tix = 0
            Rkt_chunks = []; Rv_chunks = []
            _lim = int(os.environ.get('KLIM', NCH)); _part = int(os.environ.get('KPART', 9)); _sub = int(os.environ.get('KSUB', 9))
            def chunk_vars(c):
                return (c % 4 == 3), c // 4, c % 2

            def norm_tile(c, t):
                nonlocal tix
                own, og, cb = chunk_vars(c)
                if t == 0:
                    P.op("sync", lambda e, c=c, cb=cb: e.dma_start(out=tb[cb].t[:], in_=tabK[:, 4 * c:4 * c + 4, :]),
                         writes=[tb[cb].R], dma=1)
                for t in (t,):
                    xi = tix % NX
                    sb2 = tix % 2
                    row0 = c * 512 + t * 128
                    P.op("sync", lambda e, xi=xi, row0=row0: e.dma_start(out=xt[xi].t[:], in_=xb[row0:row0 + 128, :]),
                         writes=[xt[xi].R], dma=1)
                    P.op("scalar", lambda e, xi=xi, sb2=sb2: e.activation(out=xn[sb2].t[:], in_=xt[xi].t[:], func=AF.Square,
                                                                         accum_out=ss[sb2].t[:, 0:1]),
                         reads=[xt[xi].R], writes=[xn[sb2].R, ss[sb2].R])
                    P.op("scalar", lambda e, sb2=sb2: e.activation(out=ss[sb2].t[:, 1:2], in_=ss[sb2].t[:, 0:1], func=AF.Sqrt,
                                                                   bias=EPS, scale=1.0 / D),
                         reads=[ss[sb2].R], writes=[ss[sb2].R])
                    P.op("vector", lambda e, sb2=sb2: e.reciprocal(out=ss[sb2].t[:, 1:2], in_=ss[sb2].t[:, 1:2]),
                         reads=[ss[sb2].R], writes=[ss[sb2].R])
                    P.op("vector", lambda e, xi=xi, sb2=sb2: e.tensor_scalar(out=xn[sb2].t[:], in0=xt[xi].t[:],
                                                                            scalar1=ss[sb2].t[:, 1:2], scalar2=None, op0=ALU.mult),
                         reads=[xt[xi].R, ss[sb2].R], writes=[xn[sb2].R])
                    tix += 1
                    kb = tix % 2
                    def part_b(sb2=sb2, kb=kb, cb=cb, t=t, own=own, og=og):
                        def tr8(e, sb2=sb2, kb=kb):
                            ins = None
                            for kt in range(8):
                                ins = e.transpose(out=ktp[kb].t[:, kt, :], in_=xn[sb2].t[:, kt * 128:(kt + 1) * 128],
                                                  identity=ident.t[:])
                            return ins
                        P.op("tensor", tr8, reads=[xn[sb2].R, ident.R], writes=[ktp[kb].R])
                        def ev8(e, kb=kb, cb=cb, t=t):
                            return e.tensor_tensor(out=uT[cb].t[:, :, t * 128:(t + 1) * 128], in0=ktp[kb].t[:],
                                                   in1=gt.t[:, 0:8].unsqueeze(2).to_broadcast([128, 8, 128]), op=ALU.mult)
                        P.op("vector", ev8, reads=[gt.R], writes=[uT[cb].Rs[t], ktp[kb].R])
                        if own:
                            P.op("gpsimd", lambda e, cb=cb, t=t, og=og: e.tensor_copy(
                                out=uTo.t[:, :, og * 512 + t * 128: og * 512 + (t + 1) * 128],
                                in_=uT[cb].t[:, :, t * 128:(t + 1) * 128]),
                                reads=[uT[cb].Rs[t]], writes=[uTo.Rs[og]])
                return part_b
            norm_b_pending = {}
            def proj_tile(c, t, mid=None):
                own, og, cb = chunk_vars(c)
                for t in (t,):
                    tt = c * 4 + t
                    kb = tt % 2
                    sl = slice(t * 128, (t + 1) * 128)
                    plist = [("k", kps, wkv_sb, 0), ("v", vps, wkv_sb, D)]
                    if own:
                        plist.append(("q", kps, wq_sb, 0))
                    deferred = []
                    for (nm, pst, wsb, coff) in plist:
                        if nm == "q":
                            if mid is not None:
                                mid()
                                mid = None
                            for fn_ in deferred:
                                fn_()
                            deferred = []
                        for hb in range(2):
                            def mm(e, pst=pst, wsb=wsb, coff=coff, sl=sl, cb=cb, hb=hb):
                                ins = None
                                for kt in range(8):
                                    ins = e.matmul(pst.t[:, hb * 512:(hb + 1) * 512], lhsT=uT[cb].t[:, kt, sl],
                                                   rhs=wsb.t[:, kt, coff + hb * 512: coff + (hb + 1) * 512],
                                                   start=(kt == 0), stop=(kt == 7))
                                return ins
                            P.op("tensor", mm, reads=[uT[cb].Rs[t], wsb.R], writes=[pst.Rs[hb]])
                            hs = slice(4 * hb, 4 * hb + 4)
                            pv = pst.t[:, hb * 512:(hb + 1) * 512].rearrange("p (h d) -> p h d", d=128)
                            if _sub < 2:
                                continue
                            if nm == "v":
                                P.op("scalar", lambda e, cb=cb, t=t, hs=hs, pv=pv: e.activation(
                                    out=Vc[cb].t[:, hs, t, 0:128], in_=pv, func=AF.Copy),
                                    writes=[Vc[cb].Rs[hb], pst.Rs[hb]])
                                continue
                            if _sub < 3:
                                continue
                            tbl = tb[cb].t
                            def r_a(e, pv=pv, kb=kb, t=t, tbl=tbl, hs=hs):
                                return e.tensor_tensor(out=rtmp[kb].t[:, hs, 0:32], in0=pv[:, :, 0:32],
                                                       in1=tbl[:, t, 0:32].unsqueeze(1).to_broadcast([128, 4, 32]), op=ALU.mult)
                            def r_b1(e, pv=pv, kb=kb, t=t, tbl=tbl, hs=hs):
                                return e.tensor_tensor(out=rtmp[kb].t[:, hs, 32:48], in0=pv[:, :, 16:32],
                                                       in1=tbl[:, t, 32:48].unsqueeze(1).to_broadcast([128, 4, 16]), op=ALU.mult)
                            def r_b2(e, pv=pv, kb=kb, t=t, tbl=tbl, hs=hs):
                                return e.tensor_tensor(out=rtmp[kb].t[:, hs, 48:64], in0=pv[:, :, 0:16],
                                                       in1=tbl[:, t, 48:64].unsqueeze(1).to_broadcast([128, 4, 16]), op=ALU.mult)
                            P.op("vector", r_a, reads=[tb[cb].R], writes=[rtmp[kb].Rs[3 * hb + 0], pst.Rs[hb]])
                            P.op("vector", r_b1, reads=[tb[cb].R], writes=[rtmp[kb].Rs[3 * hb + 1], pst.Rs[hb]])
                            P.op("vector", r_b2, reads=[tb[cb].R], writes=[rtmp[kb].Rs[3 * hb + 2], pst.Rs[hb]])
                            P.op("vector", lambda e, kb=kb, hs=hs: e.tensor_tensor(out=ktok[kb].t[:, hs, 0:32], in0=rtmp[kb].t[:, hs, 0:32],
                                                                                    in1=rtmp[kb].t[:, hs, 32:64], op=ALU.add),
                                 reads=rtmp[kb].Rs[3 * hb:3 * hb + 3], writes=[ktok[kb].Rs[hb]])
                            if _sub < 4:
                                continue
                            P.op("scalar", lambda e, pv=pv, kb=kb, hs=hs: e.activation(out=ktok[kb].t[:, hs, 32:128], in_=pv[:, :, 32:128],
                                                                                       func=AF.Copy),
                                 writes=[ktok[kb].Rs[2 + hb], pst.Rs[hb]])
                        if nm == "v" or _sub < 5:
                            continue
                        def post(nm=nm, kb=kb, cb=cb, sl=sl, t=t, tt=tt, c=c, og=og):
                            def tr(e):
                                ins = None
                                for h in range(H):
                                    ins = e.transpose(out=ktp[kb].t[:, h, :], in_=ktok[kb].t[:, h, :], identity=ident.t[:])
                                return ins
                            P.op("tensor", tr, reads=ktok[kb].Rs + [ident.R], writes=[ktp[kb].R])
                            if nm == "k":
                                P.op("scalar", lambda e: e.activation(out=KTc[cb].t[:, :, sl], in_=ktp[kb].t[:], func=AF.Copy),
                                     writes=[KTc[cb].Rs[t], ktp[kb].R])
                                P.op("vector", lambda e: e.tensor_reduce(out=kmeanF.t[:, :, tt:tt + 1], in_=KTc[cb].t[:, :, sl], axis=AX.X, op=ALU.add),
                                     reads=[KTc[cb].Rs[t]], writes=[kmeanF.R])
                                Rk_ = R("ktscr%d" % tt)
                                Rkt_chunks.append(Rk_)
                                P.op("gpsimd", lambda e: e.dma_start(out=kt_scr[:, :, c * 512 + t * 128:c * 512 + (t + 1) * 128], in_=KTc[cb].t[:, :, sl]),
                                     reads=[KTc[cb].Rs[t]], writes=[Rk_], dma=1, semkey=("d", "ktscr", t))
                            else:
                                osl = slice(og * 512 + t * 128, og * 512 + (t + 1) * 128)
                                P.op("scalar", lambda e: e.activation(out=QT.t[:, :, osl], in_=ktp[kb].t[:], func=AF.Copy),
                                     writes=[QT.Rs[og], ktp[kb].R])
                        deferred.append(post)
                    if mid is not None:
                        mid()
                    for fn_ in deferred:
                        fn_()
            def epilogue(c):
                own, og, cb = chunk_vars(c)
                P.op("gpsimd", lambda e, cb=cb, c=c: e.dma_start(out=v_scr[:, :, 4 * c:4 * c + 4, :], in_=Vc[cb].t[:]),
                     reads=Vc[cb].Rs + [Vc[cb].R], writes=[Rv_chunks.append(R("vscr%d" % c)) or Rv_chunks[-1]], dma=1, semkey=("d", "vscr", cb))
            NT = 4 * _lim
            LEAD = 2
            for g_ in range(min(LEAD, NT)):
                norm_tile(g_ // 4, g_ % 4)()
            for g_ in range(NT):
                pb_ = None
                if g_ + LEAD < NT:
                    pb_ = norm_tile((g_ + LEAD) // 4, (g_ + LEAD) % 4)
                proj_tile(g_ // 4, g_ % 4, pb_)
                if g_ % 4 == 3:
                    epilogue(g_ // 4)
            kmv = kmeanF.t[:].rearrange("p h (b two) -> p h b two", two=2)
            P.op("vector", lambda e: e.tensor_tensor(out=kmv[:, :, :, 0], in0=kmv[:, :, :, 0], in1=kmv[:, :, :, 1], op=ALU.add),
                 reads=[kmeanF.R], writes=[kmeanF.R])
            P.op("vector", lambda e: e.tensor_scalar(out=kmeanT.t[:], in0=kmv[:, :, :, 0], scalar1=1.0 / 256, scalar2=None, op0=ALU.mult),
                 reads=[kmeanF.R], writes=[kmeanT.R])
            if stage == "A":
                Ro = [R("o_qt"), R("o_km")]
                P.op("sync", lambda e: e.dma_start(out=o_qt[:, :, :], in_=QT.t[:]), reads=QT.Rs, writes=[Ro[0]], dma=1)
                P.op("sync", lambda e: e.dma_start(out=o_km[:, :, :], in_=kmeanF.t[:]), reads=[kmeanF.R], writes=[Ro[1]], dma=1)
                outs += [r.w for r in Ro]

        P.barrier()
        if stage != "A":
          with ExitStack() as sbk:
            mt = sb("mt", [128, 3, 16, 32], F32, stack=sbk)
            cstb = sb("cstb", [128, 256], BF16, stack=sbk)
            eo = sb("eo", [128, 32, 128], BF16, stack=sbk)
            KTh = [sb("KTh%d" % i, [128, S], BF16, stack=sbk) for i in range(2)]
            Vh = [sb("Vh%d" % i, [128, 64, VW], BF16, stack=sbk) for i in range(2)]
            gsb = sb("gsb", [128, 4, 32], F32, stack=sbk)
            mx8 = sb("mx8", [128, 4, 8], F32, stack=sbk)
            sbias = sb("sbias", [128, 4, 32], F32, stack=sbk)
            nmb = sb("nmb", [128, 4, 32], BF16, stack=sbk)
            nmT = [sb("nmT%d" % i, [128, 512], BF16, stack=sbk) for i in range(2)]
            NPT = 4
            PT = [sb("PT%d" % i, [128, 2, 512], BF16, n=2, stack=sbk) for i in range(NPT)]
            rec = sb("rec", [128, 4], F32, n=2, stack=sbk)
            atok = sb("atok", [128, 4, 128], BF16, n=4, stack=sbk)
            stp = [ps("stp%d" % i, [128, 2, 512], F32, n=2, stack=sbk) for i in range(2)]
            ob = [ps("ob%d" % i, [128, 512], F32, stack=sbk) for i in range(2)]
            gps = ps("gps", [128, 512], F32, stack=sbk)
            tps = ps("tps", [128, 2, 512], BF16, stack=sbk)

            P.op("sync", lambda e: e.dma_start(out=mt.t[:], in_=mtab[:, :, :, :]), writes=[mt.R], dma=1)
            P.op("gpsimd", lambda e: e.dma_start(out=cstb.t[:], in_=cst[:, :]), writes=[cstb.R], dma=1)
            P.op("gpsimd", lambda e: e.dma_start(out=eo.t[:], in_=eoh[:, :, :]), writes=[eo.R], dma=1)
            for q_ in range(2):
                P.op("vector", lambda e, q_=q_: e.memset(nmT[q_].t[:], 0.0), writes=[nmT[q_].R])

            def load_head(h):
                hb2 = h % 2
                P.op("sync", lambda e: e.dma_start(out=KTh[hb2].t[:], in_=kt_scr[:, h, :]), reads=Rkt_chunks, writes=[KTh[hb2].R], dma=1)
                P.op("sync", lambda e: e.dma_start(out=Vh[hb2].t[:], in_=v_scr[:, h, :, :]), reads=Rv_chunks, writes=[Vh[hb2].R], dma=1)

            _nh = int(os.environ.get('KHEADS', H))
            load_head(0)
            blkctr = 0
            def mask1(h, i):
                def gate(e):
                    ins = None
                    for qt in range(4):
                        ins = e.matmul(gps.t[:, qt * 32:(qt + 1) * 32], lhsT=QT.t[:, h, 512 * i + 128 * qt: 512 * i + 128 * (qt + 1)],
                                       rhs=kmeanT.t[:, h, :], start=True, stop=True)
                    return ins
                P.op("tensor", gate, reads=[QT.Rs[i], kmeanT.R], writes=[gps.R])
                P.op("vector", lambda e: e.tensor_tensor(out=gsb.t[:], in0=gps.t[:, 0:128].rearrange("p (a n) -> p a n", n=32),
                                                         in1=mt.t[:, 0, 4 * i:4 * i + 4, :], op=ALU.add),
                     reads=[mt.R], writes=[gsb.R, gps.R])
                for qt in range(4):
                    P.op("vector", lambda e, qt=qt: e.max(out=mx8.t[:, qt, :], in_=gsb.t[:, qt, :]), reads=[gsb.R], writes=[mx8.R])
                for qt in range(4):
                    P.op("vector", lambda e, qt=qt: e.tensor_scalar(out=sbias.t[:, qt, :], in0=gsb.t[:, qt, :], scalar1=mx8.t[:, qt, 2:3],
                                                                    scalar2=-BIG, op0=ALU.is_lt, op1=ALU.mult),
                         reads=[gsb.R, mx8.R], writes=[sbias.R])
                P.op("vector", lambda e: e.tensor_tensor(out=sbias.t[:], in0=sbias.t[:], in1=mt.t[:, 1, 4 * i:4 * i + 4, :], op=ALU.mult),
                     reads=[sbias.R, mt.R], writes=[sbias.R])
                P.op("vector", lambda e: e.tensor_tensor(out=nmb.t[:], in0=sbias.t[:], in1=mt.t[:, 2, 4 * i:4 * i + 4, :], op=ALU.add),
                     reads=[sbias.R, mt.R], writes=[nmb.R])

            def mask2(mb):
                def trm(e):
                    ins = None
                    for qt in range(4):
                        ins = e.transpose(out=tps.t[0:32, 0, qt * 128:(qt + 1) * 128], in_=nmb.t[:, qt, :], identity=ident.t[:])
                    return ins
                P.op("tensor", trm, reads=[nmb.R, ident.R], writes=[tps.R])
                P.op("scalar", lambda e: e.activation(out=nmT[mb].t[0:32, :], in_=tps.t[0:32, 0, :], func=AF.Copy),
                     writes=[nmT[mb].R, tps.R])

            def fin1():
                for b2 in range(2):
                    P.op("vector", lambda e, b2=b2: e.reciprocal(out=rec.t[:, 2 * b2:2 * b2 + 2],
                                                                 in_=ob[b2].t[:].rearrange("p (c w) -> p c w", w=256)[:, :, 128]),
                         writes=[rec.Rs[b2], ob[b2].R])
                for c in range(4):
                    P.op("vector", lambda e, c=c: e.tensor_scalar(out=atok.t[:, c, :], in0=ob[c // 2].t[:, (c % 2) * 256:(c % 2) * 256 + 128],
                                                                  scalar1=rec.t[:, c:c + 1], scalar2=None, op0=ALU.mult),
                         reads=[rec.Rs[c // 2]], writes=[atok.Rs[c], ob[c // 2].R])

            def fin2(h, i):
                def tra(e):
                    ins = None
                    for c in range(4):
                        ins = e.transpose(out=tps.t[:, 1, c * 128:(c + 1) * 128], in_=atok.t[:, c, :], identity=ident.t[:])
                    return ins
                P.op("tensor", tra, reads=atok.Rs + [ident.R], writes=[tps.R])
                P.op("scalar", lambda e: e.activation(out=attnT.t[:, h, 512 * i:512 * i + 512], in_=tps.t[:, 1, :], func=AF.Copy),
                     writes=[attnT.Rs[i], tps.R])

            groups = [(h, i) for h in range(_nh) for i in range(int(os.environ.get('KGROUPS', 4)))]
            mask1(*groups[0])
            mask2(0)
            for gi, (h, i) in enumerate(groups):
                if i == 0 and h + 1 < _nh:
                    load_head(h + 1)
                hb2 = h % 2
                if True:
                    qsl = slice(512 * i, 512 * i + 512)
                    nb = 8 * i + 8
                    mb = gi % 2
                    def emit_qk(n, sbuf, h=h, i=i, qsl=qsl, hb2=hb2, mb=mb):
                        def f(e):
                            ins = None
                            for a in range(2):
                                kt = 2 * n + a
                                own = n >= 8 * i + 6
                                c0 = 256 if n == 8 * i + 7 else 0
                                dst = stp[sbuf].t[:, a, c0:512]
                                e.matmul(dst, lhsT=KTh[hb2].t[:, kt * 128:(kt + 1) * 128], rhs=QT.t[:, h, 512 * i + c0:512 * i + 512], start=True, stop=False)
                                ins = e.matmul(dst, lhsT=eo.t[:, n, :], rhs=nmT[mb].t[:, c0:512], start=False, stop=(not own))
                                if own:
                                    ar = 2 * (n - (8 * i + 6)) + a
                                    if ar % 2 == 1:
                                        e.matmul(stp[sbuf].t[:, a, (ar - 1) * 128: ar * 128], lhsT=ident.t[:], rhs=cstb.t[:, 128:256],
                                                 start=False, stop=False)
                                    ins = e.matmul(stp[sbuf].t[:, a, ar * 128:(ar + 1) * 128], lhsT=ident.t[:], rhs=cstb.t[:, 0:128],
                                                   start=False, stop=True)
                            return ins
                        P.op("tensor", f, reads=[KTh[hb2].R, QT.Rs[i], eo.R, nmT[mb].R, ident.R, cstb.R], writes=[stp[sbuf].Rs[0], stp[sbuf].Rs[1]])

                    def emit_exp(n, sbuf, pb, i=i):
                        c0 = 256 if n == 8 * i + 7 else 0
                        for a in range(2):
                            P.op("scalar", lambda e, a=a: e.activation(out=PT[pb].t[:, a, c0:512], in_=stp[sbuf].t[:, a, c0:512], func=AF.Exp, scale=SCALE),
                                 writes=[PT[pb].Rs[a], stp[sbuf].Rs[a]])

                    def emit_pv(n, pb, h=h, i=i, hb2=hb2, nb=nb):
                        def f(e):
                            ins = None
                            for a in range(2):
                                kt = 2 * n + a
                                ar = 2 * (n - (8 * i + 6)) + a if n >= 8 * i + 6 else -1
                                for c in range(4):
                                    if c < ar:
                                        continue
                                    first = (n == 0 and a == 0)
                                    last = (n == nb - 1 and a == 1) or (n == nb - 1 and a == 0 and c < 2 * (n - (8 * i + 6)) + 1)
                                    ins = e.matmul(ob[c // 2].t[:, (c % 2) * 256:(c % 2) * 256 + 129], lhsT=PT[pb].t[:, a, c * 128:(c + 1) * 128],
                                                   rhs=Vh[hb2].t[:, kt, 0:129], start=(first and c % 2 == 0), stop=last, skip_group_check=True)
                            return ins
                        P.op("tensor", f, reads=[PT[pb].Rs[0], PT[pb].Rs[1], Vh[hb2].R], writes=[ob[0].R, ob[1].R])

                    pend = []
                    for n in range(nb):
                        sbuf = blkctr % 2
                        pb = blkctr % NPT
                        blkctr += 1
                        emit_qk(n, sbuf)
                        emit_exp(n, sbuf, pb)
                        pend.append((n, pb))
                        if len(pend) > 2:
                            emit_pv(*pend.pop(0))
                        if n == 0 and gi + 1 < len(groups):
                            mask1(*groups[gi + 1])
                        if n == 2 and gi >= 1:
                            fin2(*groups[gi - 1])
                        if n == 4 and gi + 1 < len(groups):
                            mask2((gi + 1) % 2)
                    while pend:
                        emit_pv(*pend.pop(0))
                    fin1()
            fin2(*groups[-1])
            if stage == "B":
                Rq = R("o_qt2")
                P.op("sync", lambda e: e.dma_start(out=o_qt2[:, :, :], in_=QT.t[:]), reads=QT.Rs, writes=[Rq], dma=1)
                outs.append(Rq.w)
                Rd = [R("dbg%d" % q) for q in range(3)]
                P.op("sync", lambda e: e.dma_start(out=o_d1[:, :, :], in_=gsb.t[:]), reads=[gsb.R], writes=[Rd[0]], dma=1)
                P.op("sync", lambda e: e.dma_start(out=o_d2[:, :, :], in_=nmb.t[:]), reads=[nmb.R], writes=[Rd[1]], dma=1)
                P.op("sync", lambda e: e.dma_start(out=o_d3[:, :], in_=nmT[0].t[0:32, :]), reads=[nmT[0].R], writes=[Rd[2]], dma=1)
                outs += [r.w for r in Rd]
                Ro = R("o_at")
                P.op("sync", lambda e: e.dma_start(out=o_at[:, :, :], in_=attnT.t[:]), reads=attnT.Rs, writes=[Ro], dma=1)
                outs.append(Ro.w)

        if stage == "full":
          P.barrier()
          with ExitStack() as sc:
            M1 = sb("M1", [128, 8 * 4 * 514], BF16, stack=sc)
            M2 = sb("M2", [128, 8 * 2048], BF16, stack=sc)
            z = M1.t[:, :].rearrange("p (f g t) -> p f g t", f=8, g=4)
            mrg = M1.t[:, 0:8 * 2048].rearrange("p (f t) -> p f t", f=8)
            zc = M2.t[:, :].rearrange("p (f t) -> p f t", f=8)
            zR = [R("z%d" % q) for q in range(8)]; zcR = [R("zc%d" % q) for q in range(8)]; mR = [R("mrg%d" % q) for q in range(8)]
            hview = UT.t[:, 0:2 * 16 * D].bitcast(F32).rearrange("p (t d) -> p t d", t=16)
            hR = [R("h%d" % q) for q in range(16)]
            wdn = QTW.t[:, :].rearrange("p (k c) -> p k c", k=22); wdnR = R("wdn")
            NW = 2
            WBC = 256
            wb = [sb("wb%d" % q, [128, 8, WBC], BF16, stack=sc) for q in range(NW)]
            NWX = 5
            wbx = []
            for q_ in range(NWX):
                v_ = _V(); v_.t = QTW.t[:, q_ * 8 * WBC:(q_ + 1) * 8 * WBC].rearrange("p (k c) -> p k c", k=8); v_.R = R("wbx%d" % q_); v_.name = "wbx%d" % q_
                wbx.append(v_)
            wring = wb + wbx
            gf = sb("gf", [128, D], F32, stack=sc)
            ctmp = sb("ctmp", [128, 512], F32, stack=sc)
            sg = [sb("sg%d" % q, [128, 512], F32, stack=sc) for q in range(2)]
            xo = [sb("xo%d" % q, [128, 256], F32, stack=sc) for q in range(2)]
            xhn = sb("xhn", [8, D], BF16, stack=sc)
            st8 = sb("st8", [8, 2], F32, stack=sc)
            ssc = [sb("ssc%d" % q, [128, 2], F32, stack=sc) for q in range(2)]
            xnc = [sb("xnc0", [128, D], BF16, stack=sc)] * 2
            ot = [sb("ot0", [128, D], F32, stack=sc)] * 2
            xhs = _V(); xhs.t = ot[0].t[0:8, :]; xhs.R = ot[0].R
            xhq = xhn
            xnc2 = _V(); xnc2.t = ctmp.t[:, :].bitcast(BF16); xnc2.R = R("xnc_alt")
            xnc = [xnc[0], xnc2]
            NPS = 6
            pp = [ps("pp%d" % q, [128, 512], F32, stack=sc) for q in range(NPS)]
            tpc = [ps("tpc%d" % q, [128, 8, 128], BF16, stack=sc) for q in range(2)]
            cnt = {"w": 0, "p": 0, "x": 0, "s": 0}

            def nxt(k, n):
                v = cnt[k] % n
                cnt[k] += 1
                return v

            def stream(wd, col0, ncols=256, dst0=0, buf=None):
                if buf is None:
                    buf = wring[nxt("w", len(wring))]
                P.op("gpsimd", lambda e: e.dma_start(out=buf.t[:, :, dst0:dst0 + ncols],
                                                     in_=wd[:, col0:col0 + ncols].rearrange("(k p) c -> p k c", p=128)),
                     writes=[buf.R], dma=1)
                return buf

            def proj(buf, c0, act, actR, sl, n=512):
                p = pp[nxt("p", NPS)]
                def f(e):
                    ins = None
                    for kt in range(8):
                        ins = e.matmul(p.t[:, 0:n], lhsT=buf.t[:, kt, c0:c0 + 128], rhs=act[:, kt, sl], start=(kt == 0), stop=(kt == 7))
                    return ins
                P.op("tensor", f, reads=[buf.R] + list(actR), writes=[p.R])
                return p

            P.op("sync", lambda e: e.dma_start(out=gf.t[:], in_=gfin[:, :]), writes=[gf.R], dma=1)
            P.op("sync", lambda e: e.dma_start(out=xhs.t[:], in_=xh[:, :]), writes=[xhs.R], dma=1)
            P.op("scalar", lambda e: e.activation(out=xhq.t[:], in_=xhs.t[:], func=AF.Square, accum_out=st8.t[:, 0:1]), reads=[xhs.R], writes=[xhq.R, st8.R])
            P.op("scalar", lambda e: e.activation(out=st8.t[:, 1:2], in_=st8.t[:, 0:1], func=AF.Sqrt, bias=EPS, scale=1.0 / D), reads=[st8.R], writes=[st8.R])
            P.op("vector", lambda e: e.reciprocal(out=st8.t[:, 1:2], in_=st8.t[:, 1:2]), reads=[st8.R], writes=[st8.R])
            P.op("vector", lambda e: e.tensor_scalar(out=xhn.t[:], in0=xhs.t[:], scalar1=st8.t[:, 1:2], scalar2=None, op0=ALU.mult), reads=[xhs.R, st8.R], writes=[xhn.R])
            def trh(e):
                ins = None
                for kt in range(8):
                    ins = e.transpose(out=tpc[0].t[:, kt, 0:8], in_=xhn.t[:, kt * 128:(kt + 1) * 128], identity=ident.t[0:8, 0:8])
                return ins
            P.op("tensor", trh, reads=[xhn.R, ident.R], writes=[tpc[0].R])
            P.op("vector", lambda e: e.tensor_tensor(out=uTo.t[:, :, 2048:2056], in0=tpc[0].t[:, :, 0:8],
                                                     in1=gt.t[:, 0:8].unsqueeze(2).to_broadcast([128, 8, 8]), op=ALU.mult),
                 reads=[gt.R], writes=[uTo.Rs[4], tpc[0].R])

            chunks = [(slice(512 * g, 512 * g + 512), 512, g) for g in range(4)] + [(slice(2048, 2056), 8, 4)]
            def c1_proj(pname, col, cbk):
                buf = stream(wrest, col + cbk * 256)
                for f4 in range(2):
                    ft = cbk * 2 + f4
                    for (sl, n, g) in (chunks if pname != "bg" else chunks[:4]):
                        p = proj(buf, f4 * 128, uTo.t, [uTo.Rs[g]], sl, n)
                        if pname == "bg":
                            P.op("vector", lambda e, p=p, ft=ft, sl=sl: e.tensor_tensor(out=zc[:, ft, sl], in0=p.t[:], in1=zc[:, ft, sl], op=ALU.mult),
                                 reads=[zcR[ft]], writes=[zcR[ft], p.R])
                            continue
                        dst = z[:, ft, g, 2:514] if g < 4 else z[:, ft, :, 0:2]
                        src = p.t[:, 0:n] if g < 4 else p.t[:, 0:8].rearrange("p (g t) -> p g t", t=2)
                        if pname == "xc":
                            P.op("scalar", lambda e, dst=dst, src=src: e.activation(out=dst, in_=src, func=AF.Copy), writes=[zR[ft], p.R])
                        else:
                            P.op("vector", lambda e, dst=dst, src=src: e.tensor_tensor(out=dst, in0=src, in1=dst, op=ALU.mult), writes=[zR[ft], p.R])

            def conv_ft(ft):
                for g in range(4):
                    zt = z[:, ft, g, :]
                    zo = zc[:, ft, 512 * g:512 * g + 512]
                    P.op("vector", lambda e, zt=zt: e.tensor_scalar(out=ctmp.t[:], in0=zt[:, 2:514], scalar1=gt.t[:, 32 + ft:33 + ft], scalar2=None, op0=ALU.mult),
                         reads=[zR[ft], gt.R], writes=[ctmp.R])
                    P.op("vector", lambda e, zt=zt: e.scalar_tensor_tensor(out=ctmp.t[:], in0=zt[:, 1:513], scalar=gt.t[:, 24 + ft:25 + ft], in1=ctmp.t[:], op0=ALU.mult, op1=ALU.add),
                         reads=[zR[ft], gt.R, ctmp.R], writes=[ctmp.R])
                    P.op("vector", lambda e, zt=zt, zo=zo: e.scalar_tensor_tensor(out=zo, in0=zt[:, 0:512], scalar=gt.t[:, 16 + ft:17 + ft], in1=ctmp.t[:], op0=ALU.mult, op1=ALU.add),
                         reads=[zR[ft], gt.R, ctmp.R], writes=[zcR[ft]])

            for cbk in range(4):
                c1_proj("xc", 2 * D, cbk)
            for cbk in range(4):
                c1_proj("cg", 0, cbk)
                conv_ft(2 * cbk)
                conv_ft(2 * cbk + 1)
                if cbk >= 1:
                    c1_proj("bg", D, cbk - 1)
            c1_proj("bg", D, 3)
            for (gcol, wd, act, actRs, first) in ((4 * D, w_cb, zc, zcR, True), (3 * D, w_ab, attnT.t, None, False)):
                for cbk in range(8):
                    bg = wring[nxt("w", len(wring))]
                    stream(wrest, gcol + cbk * 128, 128, 0, bg)
                    stream(wd, cbk * 128, 128, 128, bg)
                    bw = bg
                    for f4 in range(1):
                        dtile = cbk
                        for (sl, n, g) in chunks[:4]:
                            pg = proj(bg, 0, uTo.t, [uTo.Rs[g]], sl)
                            py = proj(bw, 128, act, (actRs if actRs is not None else [attnT.Rs[g]]), sl)
                            s_ = sg[nxt("s", 2)]
                            P.op("scalar", lambda e, pg=pg, s_=s_: e.activation(out=s_.t[:], in_=pg.t[:], func=AF.Sigmoid), writes=[s_.R, pg.R])
                            if first:
                                P.op("vector", lambda e, py=py, s_=s_, dtile=dtile, sl=sl: e.tensor_tensor(out=mrg[:, dtile, sl], in0=py.t[:], in1=s_.t[:], op=ALU.mult),
                                     reads=[s_.R] + zR, writes=[mR[dtile], py.R])
                            else:
                                P.op("vector", lambda e, py=py, s_=s_: e.tensor_tensor(out=s_.t[:], in0=py.t[:], in1=s_.t[:], op=ALU.mult),
                                     reads=[s_.R], writes=[s_.R, py.R])
                                P.op("vector", lambda e, s_=s_, dtile=dtile, sl=sl: e.tensor_tensor(out=mrg[:, dtile, sl], in0=mrg[:, dtile, sl], in1=s_.t[:], op=ALU.add),
                                     reads=[s_.R], writes=[mR[dtile]])
            P.barrier()
            for cbk in range(4):
                buf = stream(w_o, cbk * 256)
                for tt in range(16):
                    row0 = (4 * (tt // 4) + 3) * 512 + (tt % 4) * 128
                    xb_ = xo[nxt("x", 2)]
                    P.op("sync", lambda e, xb_=xb_, row0=row0, cbk=cbk: e.dma_start(out=xb_.t[:], in_=xb[row0:row0 + 128, cbk * 256:(cbk + 1) * 256]),
                         writes=[xb_.R], dma=1)
                    p = pp[nxt("p", NPS)]
                    def f(e, p=p, tt=tt, buf=buf):
                        ins = None
                        for kt in range(8):
                            ins = e.matmul(p.t[:, 0:256], lhsT=mrg[:, kt, tt * 128:(tt + 1) * 128], rhs=buf.t[:, kt, :], start=(kt == 0), stop=(kt == 7))
                        return ins
                    P.op("tensor", f, reads=[buf.R] + mR, writes=[p.R])
                    P.op("vector", lambda e, p=p, xb_=xb_, tt=tt, cbk=cbk: e.tensor_tensor(out=hview[:, tt, cbk * 256:(cbk + 1) * 256], in0=p.t[:, 0:256], in1=xb_.t[:], op=ALU.add),
                         reads=[xb_.R], writes=[hR[tt], p.R])
            P.barrier()
            u2T = M1.t[:, 0:8 * 1024].rearrange("p (k t) -> p k t", k=8); u2R = [R("u2T%d" % q) for q in range(8)]
            MM = M2.t[:, :]
            aT_a = M2.t[:, 0:16 * 1024].rearrange("p (f t) -> p f t", f=16)
            aT_b = M1.t[:, 8 * 1024:14 * 1024].rearrange("p (f t) -> p f t", f=6)
            aR = [R("aT%d" % q) for q in range(22)]
            def aT(f):
                return aT_a[:, f, :] if f < 16 else aT_b[:, f - 16, :]
            w3 = _V(); w3.t = M1.t[:, 14 * 1024:14 * 1024 + 8 * WBC].rearrange("p (k c) -> p k c", k=8); w3.R = R("wb_m1"); w3.name = "wb_m1"
            wff = wb + [w3]

            def rms(tt, srcR):
                s2 = ssc[tt % 2]
                sqc = xnc[tt % 2]
                P.op("scalar", lambda e: e.activation(out=sqc.t[:], in_=hview[:, tt, :], func=AF.Square, accum_out=s2.t[:, 0:1]), reads=[srcR], writes=[sqc.R, s2.R])
                P.op("scalar", lambda e: e.activation(out=s2.t[:, 1:2], in_=s2.t[:, 0:1], func=AF.Sqrt, bias=EPS, scale=1.0 / D), reads=[s2.R], writes=[s2.R])
                P.op("vector", lambda e: e.reciprocal(out=s2.t[:, 1:2], in_=s2.t[:, 1:2]), reads=[s2.R], writes=[s2.R])
                return s2

            for hf in range(2):
                for t8 in range(8):
                    tt = hf * 8 + t8
                    s2 = rms(tt, hR[tt])
                    xn_ = xnc[tt % 2]
                    P.op("vector", lambda e, tt=tt, s2=s2, xn_=xn_: e.tensor_scalar(out=xn_.t[:], in0=hview[:, tt, :], scalar1=s2.t[:, 1:2], scalar2=None, op0=ALU.mult),
                         reads=[hR[tt], s2.R], writes=[xn_.R])
                    tp_ = tpc[tt % 2]
                    def tr8(e, xn_=xn_, tp_=tp_):
                        ins = None
                        for kt in range(8):
                            ins = e.transpose(out=tp_.t[:, kt, :], in_=xn_.t[:, kt * 128:(kt + 1) * 128], identity=ident.t[:])
                        return ins
                    P.op("tensor", tr8, reads=[xn_.R, ident.R], writes=[tp_.R])
                    P.op("vector", lambda e, tp_=tp_, t8=t8: e.tensor_tensor(out=u2T[:, :, t8 * 128:(t8 + 1) * 128], in0=tp_.t[:],
                                                                             in1=gt.t[:, 8:16].unsqueeze(2).to_broadcast([128, 8, 128]), op=ALU.mult),
                         reads=[gt.R], writes=[u2R[t8], tp_.R])
                for fb in range(22):
                    buf = wff[nxt("w", len(wff))]
                    stream(w_gu, fb * 128, 128, 0, buf)
                    stream(w_gu, DFF + fb * 128, 128, 128, buf)
                    if hf == 0:
                        P.op("gpsimd", lambda e, kt=fb: e.dma_start(out=wdn[:, kt, :], in_=w_dn[kt * 128:(kt + 1) * 128, :]),
                             writes=[wdnR] + [w_.R for w_ in wbx], dma=1)
                    for f2 in range(1):
                        f_ = fb
                        for c2 in range(2):
                            sl = slice(c2 * 512, (c2 + 1) * 512)
                            rr = u2R[4 * c2:4 * c2 + 4]
                            pg = proj(buf, 0, u2T, rr, sl)
                            pu = proj(buf, 128, u2T, rr, sl)
                            s_ = sg[nxt("s", 2)]
                            P.op("scalar", lambda e, pg=pg, s_=s_: e.activation(out=s_.t[:], in_=pg.t[:], func=AF.Silu), writes=[s_.R, pg.R])
                            P.op("vector", lambda e, pu=pu, s_=s_, f_=f_, sl=sl: e.tensor_tensor(out=aT(f_)[:, sl], in0=pu.t[:], in1=s_.t[:], op=ALU.mult),
                                 reads=[s_.R], writes=[aR[f_], pu.R])
                for t8 in range(8):
                    tt = hf * 8 + t8
                    for nb2 in range(2):
                        p = pp[nxt("p", NPS)]
                        def f(e, p=p, t8=t8, nb2=nb2):
                            ins = None
                            for kt in range(22):
                                ins = e.matmul(p.t[:], lhsT=aT(kt)[:, t8 * 128:(t8 + 1) * 128], rhs=wdn[:, kt, nb2 * 512:(nb2 + 1) * 512],
                                               start=(kt == 0), stop=(kt == 21))
                            return ins
                        P.op("tensor", f, reads=[wdnR] + aR, writes=[p.R])
                        P.op("vector", lambda e, p=p, tt=tt, nb2=nb2: e.tensor_tensor(out=hview[:, tt, nb2 * 512:(nb2 + 1) * 512], in0=p.t[:],
                                                                                      in1=hview[:, tt, nb2 * 512:(nb2 + 1) * 512], op=ALU.add),
                             writes=[hR[tt], p.R])
                    s2 = rms(tt, hR[tt])
                    o_ = ot[tt % 2]
                    P.op("vector", lambda e, tt=tt, s2=s2, o_=o_: e.scalar_tensor_tensor(out=o_.t[:], in0=hview[:, tt, :], scalar=s2.t[:, 1:2], in1=gf.t[:],
                                                                                         op0=ALU.mult, op1=ALU.mult),
                         reads=[hR[tt], s2.R, gf.R], writes=[o_.R])
                    Ro_ = R("out%d" % tt)
                    P.op("sync", lambda e, o_=o_, tt=tt: e.dma_start(out=out_d[tt * 128:(tt + 1) * 128, :], in_=o_.t[:]), reads=[o_.R], writes=[Ro_], dma=1,
                         semkey=("d", "out"))
                    outs.append(Ro_.w)
                if hf == 0:
                    pass
        for kk in [("d", "ktscr", q_) for q_ in range(4)] + [("d", "vscr", 0), ("d", "vscr", 1)]:
            if kk in P.count:
                outs.append((kk, P.count[kk], "sync", True))
        P.wait_all("sync", outs)
        P.emit()
    return nc


def slot_groups(j):
    order = []
    for i in range(4):
        row = [4 * i + r for r in range(4)]
        order += [g for g in row if g != 4 * i + j] + [4 * i + j]
    return order


def rope_table(positions):
    rd = DH // 4
    inv = (ROPE_THETA ** (-np.arange(0, rd, 2, dtype=np.float32) / rd)).astype(np.float32)
    ang = positions.astype(np.float32)[:, None] * inv[None, :]
    cos, sin = np.cos(ang).astype(np.float32), np.sin(ang).astype(np.float32)
    return np.concatenate([cos, cos, -sin, sin], axis=1)


def core_inputs(c, x, g_mix, w_in, conv_w, g_ffn, g_final=None, shared=None):
    b, j = c // 4, c % 4
    order = slot_groups(j)
    tok = np.concatenate([np.arange(g * 512, (g + 1) * 512) for g in order])
    xbs = np.ascontiguousarray(x[b][tok])
    tab = rope_table(tok)
    tabK = np.ascontiguousarray(tab.reshape(64, 128, 64).transpose(1, 0, 2))
    gtab = np.zeros((128, 40), np.float32)
    gtab[:, 0:8] = g_mix[0].reshape(8, 128).T
    gtab[:, 8:16] = g_ffn[0].reshape(8, 128).T
    for tap in range(3):
        gtab[:, 16 + 8 * tap:24 + 8 * tap] = conv_w[0][tap].reshape(8, 128).T
    xh = np.zeros((8, D), np.float32)
    for i in range(4):
        g0 = (4 * i + j) * 512
        if g0 >= 2:
            xh[2 * i:2 * i + 2] = x[b][g0 - 2:g0]
    mt = np.zeros((3, 16, 32), np.float32)
    for i in range(4):
        for qt in range(4):
            own_blk = 8 * i + 6 + qt // 2
            my_blk = 2 * (4 * i + j) + qt // 2
            for n in range(32):
                actual_blk = 2 * order[n // 2] + n % 2
                past = actual_blk < my_blk
                is_own = (n == own_blk)
                mt[0, 4 * i + qt, n] = 0.0 if past else NEG
                mt[1, 4 * i + qt, n] = 1.0 if past else 0.0
                mt[2, 4 * i + qt, n] = 0.0 if (past or is_own) else -BIG
    mtab = np.ascontiguousarray(np.broadcast_to(mt[None], (128, 3, 16, 32))).astype(np.float32)
    m = {"xb": xbs, "tabK": tabK, "gtab": gtab, "mtab": mtab, "xh": xh}
    if shared is not None:
        m.update(shared)
    else:
        m.update(shared_inputs(w_in, g_final))
    return m, order


def shared_inputs(w_in, g_final=None, w_attn_branch=None, w_conv_branch=None, w_out=None, w_gate_up=None, w_down=None):
    k = np.arange(128)[:, None]
    q = np.arange(128)[None, :]
    cst = np.zeros((128, 256), np.float32)
    cst[:, 0:128] = np.where(k > q, -BIG, 0.0)
    cst[:, 128:256] = -BIG
    eoh = np.zeros((128, 32, 128), np.float32)
    for n in range(32):
        eoh[n, n, :] = 1.0
    sh = {"wkv": np.ascontiguousarray(w_in[0][:, D:3 * D]), "wq": np.ascontiguousarray(w_in[0][:, 0:D]), "cst": cst, "eoh": eoh}
    if w_out is not None:
        sh.update({
            "wrest": np.ascontiguousarray(w_in[0][:, 3 * D:8 * D]),
            "w_ab": np.ascontiguousarray(w_attn_branch[0]), "w_cb": np.ascontiguousarray(w_conv_branch[0]),
            "w_o": np.ascontiguousarray(w_out[0]), "w_gu": np.ascontiguousarray(w_gate_up[0]), "w_dn": np.ascontiguousarray(w_down[0]),
            "gfin": np.ascontiguousarray(np.broadcast_to(np.asarray(g_final, np.float32)[None, :], (128, D))),
        })
    return sh


_NC_CACHE = {}


def kernel(x, g_mix, w_in, conv_w, w_attn_branch, w_conv_branch, w_out, g_ffn, w_gate_up, w_down, g_final):
    args = [np.asarray(a, dtype=np.float32) for a in (x, g_mix, w_in, conv_w, w_attn_branch, w_conv_branch, w_out, g_ffn,
                                                       w_gate_up, w_down, g_final)]
    x, g_mix, w_in, conv_w, w_attn_branch, w_conv_branch, w_out, g_ffn, w_gate_up, w_down, g_final = args
    if "nc" not in _NC_CACHE:
        _NC_CACHE["nc"] = build("full")
    nc = _NC_CACHE["nc"]
    sh = shared_inputs(w_in, g_final, w_attn_branch, w_conv_branch, w_out, w_gate_up, w_down)
    maps, orders = [], []
    for c in range(8):
        m, o = core_inputs(c, x, g_mix, w_in, conv_w, g_ffn, g_final, shared=sh)
        maps.append(m)
        orders.append(o)
    res = run_bass_kernel_spmd(nc, maps, core_ids=list(range(8)))
    out = np.zeros((2, S, D), np.float32)
    for c in range(8):
        b, j = c // 4, c % 4
        o = np.asarray(res.results[c]["out"], dtype=np.float32)
        for i in range(4):
            g = 4 * i + j
            out[b, g * 512:(g + 1) * 512] = o[i * 512:(i + 1) * 512]
    return out
```

```python
from contextlib import ExitStack
import os
import numpy as np
import concourse.bass as bass
import concourse.mybir as mybir
from concourse.bass_utils import run_bass_kernel_spmd

F32 = mybir.dt.float32
BF16 = mybir.dt.bfloat16
AF = mybir.ActivationFunctionType
ALU = mybir.AluOpType
AX = mybir.AxisListType

D = 1024
S = 8192
H = 8
DH = 128
NCH = 16
NBLK = 32
DFF = 2816
EPS = 1e-6
NEG = -1e30
BIG = 30000.0
SCALE = 1.0 / float(np.sqrt(DH))
ROPE_THETA = 500000.0
VW = 132

ENGINES = ("sync", "scalar", "vector", "gpsimd", "tensor")
STRICT_SAME_ENGINE = False


class R:
    __slots__ = ("name", "w", "r")

    def __init__(self, name):
        self.name = name
        self.w = None
        self.r = {}


class Prog:
    def __init__(self, nc):
        self.nc = nc
        self.ops = {e: [] for e in ENGINES}
        self.semh = {}
        self.count = {}
        self.waited = {e: {} for e in ENGINES}
        self.stack = ExitStack()

    def sem(self, key):
        if key not in self.semh:
            self.semh[key] = self.stack.enter_context(self.nc.semaphore("s%d" % len(self.semh)))
            self.count[key] = 0
        return self.semh[key]

    def _need(self, eng, waits, tok, kind):
        if tok is None:
            return
        key, val, peng, is_dma = tok
        if (not is_dma) and peng == eng and (eng == "tensor" or (kind != "RAW" and not STRICT_SAME_ENGINE)):
            return
        if self.waited[eng].get(key, 0) >= val:
            return
        if waits.get(key, 0) < val:
            waits[key] = val

    def op(self, eng, fn, reads=(), writes=(), dma=0, semkey=None):
        waits = {}
        for r in reads:
            self._need(eng, waits, r.w, "RAW")
        for w in writes:
            self._need(eng, waits, w.w, "WAW")
            for tok in w.r.values():
                self._need(eng, waits, tok, "WAR")
        for k, v in waits.items():
            self.waited[eng][k] = v
        if dma:
            key = semkey if semkey is not None else ("d", writes[0].name)
            self.sem(key)
            self.count[key] += 16 * dma
            tok = (key, self.count[key], eng, True)
        else:
            key = ("e", eng)
            self.sem(key)
            self.count[key] += 1
            tok = (key, self.count[key], eng, False)
        for w in writes:
            w.w = tok
            w.r = {}
        for r in reads:
            r.r[key] = tok
        self.ops[eng].append((list(waits.items()), fn, key, bool(dma), dma))
        return tok

    def wait_all(self, eng, toks):
        waits = {}
        for tok in toks:
            key, val, _, _ = tok
            if waits.get(key, 0) < val:
                waits[key] = val
        self.ops[eng].append((list(waits.items()), None, None, False, 0))

    def barrier(self):
        for eng in ENGINES:
            w = [(k, v) for k, v in self.count.items() if v > 0 and self.waited[eng].get(k, 0) < v]
            for k, v in w:
                self.waited[eng][k] = v
            self.ops[eng].append((w, None, None, False, 0))

    def emit(self):
        nc = self.nc
        with nc.Block() as block:
            for eng in ENGINES:
                ops = self.ops[eng]
                if not ops:
                    continue

                def body(e, ops=ops):
                    for waits, fn, key, is_dma, ndma in ops:
                        for k, v in waits:
                            e.wait_ge(self.semh[k], v)
                        if fn is None:
                            continue
                        ins = fn(e)
                        if is_dma:
                            if not isinstance(ins, (list, tuple)):
                                ins = [ins]
                            assert len(ins) == ndma, (len(ins), ndma)
                            for i in ins:
                                i.then_inc(self.semh[key], 16)
                        else:
                            if isinstance(ins, (list, tuple)):
                                ins = ins[-1]
                            ins.then_inc(self.semh[key], 1)

                getattr(block, eng)(body)


class Buf:
    def __init__(self, t, name, n=1):
        self.t = t
        self.R = R(name)
        self.Rs = [R("%s_%d" % (name, i)) for i in range(n)]


def build(stage="full"):
    nc = bass.Bass("TRN2", target_bir_lowering=False)
    dt = lambda name, shape, dtype=F32, kind="ExternalInput": nc.dram_tensor(name, shape, dtype, kind=kind)
    xb = dt("xb", [S, D])
    tabK = dt("tabK", [128, 64, 64])
    wkv = dt("wkv", [D, 2 * D])
    wq = dt("wq", [D, D])
    gtab = dt("gtab", [128, 40])
    xh = dt("xh", [8, D])
    gfin = dt("gfin", [128, D])
    wrest = dt("wrest", [D, 5 * D])
    w_ab = dt("w_ab", [D, D]); w_cb = dt("w_cb", [D, D]); w_o = dt("w_o", [D, D])
    w_gu = dt("w_gu", [D, 2 * DFF]); w_dn = dt("w_dn", [DFF, D])
    mtab = dt("mtab", [128, 3, 16, 32])
    cst = dt("cst", [128, 256])
    eoh = dt("eoh", [128, 32, 128])
    if stage == "A":
        kt_scr = dt("kt_scr", [128, H, S], BF16, "ExternalOutput")
        v_scr = dt("v_scr", [128, H, 64, VW], BF16, "ExternalOutput")
        o_qt = dt("o_qt", [128, H, 2048], BF16, "ExternalOutput")
        o_km = dt("o_km", [128, H, NBLK], F32, "ExternalOutput")
    else:
        if stage == "full":
            out_d = dt("out", [2048, D], F32, "ExternalOutput")
        if stage == "B":
            o_at = dt("o_at", [128, H, 2048], BF16, "ExternalOutput")
            o_qt2 = dt("o_qt2", [128, H, 2048], BF16, "ExternalOutput")
            o_d1 = dt("o_d1", [128, 4, 32], F32, "ExternalOutput")
            o_d2 = dt("o_d2", [128, 4, 32], BF16, "ExternalOutput")
            o_d3 = dt("o_d3", [32, 512], BF16, "ExternalOutput")
        kt_scr = nc.dram_tensor("kt_scr", [128, H, S], BF16)
        v_scr = nc.dram_tensor("v_scr", [128, H, 64, VW], BF16)

    P = Prog(nc)
    outs = []
    with P.stack:
        st = P.stack

        def sb(name, shape, dtype, n=1, stack=st):
            return Buf(stack.enter_context(nc.sbuf_tensor(name, shape, dtype)), name, n)

        def ps(name, shape, dtype, n=1, stack=st):
            return Buf(stack.enter_context(nc.psum_tensor(name, shape, dtype)), name, n)

        ident = sb("ident", [128, 128], BF16)
        identf = sb("identf", [128, 128], F32)
        gt = sb("gt", [128, 40], F32)
        kmeanT = sb("kmeanT", [128, H, NBLK], BF16)
        kmeanF = sb("kmeanF", [128, H, 2 * NBLK], F32)
        QTW = sb("QTW", [128, 22 * D], BF16, n=4)
        UT = sb("UT", [128, 8 * 2056 + 8 * 2048], BF16, n=4)
        class _V:
            pass
        QT = _V(); QT.t = QTW.t[:, 0:H * 2048].rearrange("p (h t) -> p h t", h=H); QT.Rs = QTW.Rs; QT.R = QTW.R
        uTo = _V(); uTo.t = UT.t[:, 0:8 * 2056].rearrange("p (k t) -> p k t", k=8); uTo.Rs = [R("uTo%d" % q) for q in range(5)]; uTo.R = R("uTo")
        attnT = _V(); attnT.t = UT.t[:, 8 * 2056:8 * 2056 + 8 * 2048].rearrange("p (h t) -> p h t", h=H); attnT.Rs = [R("attnT%d" % q) for q in range(4)]

        P.op("gpsimd", lambda e: e.memset(identf.t[:], 1.0), writes=[identf.R])
        P.op("gpsimd", lambda e: e.affine_select(out=identf.t[:], in_=identf.t[:], pattern=[[-1, 128]],
                                                 compare_op=ALU.is_equal, fill=0.0, base=0, channel_multiplier=1),
             reads=[identf.R], writes=[identf.R])
        P.op("vector", lambda e: e.tensor_copy(out=ident.t[:], in_=identf.t[:]), reads=[identf.R], writes=[ident.R])
        P.op("sync", lambda e: e.dma_start(out=gt.t[:], in_=gtab[:, :]), writes=[gt.R], dma=1)

        with ExitStack() as sa:
            wkv_sb = sb("wkv_sb", [128, 8, 2 * D], BF16, stack=sa)
            wq_sb = _V(); wq_sb.t = UT.t[:, 8 * 2056:8 * 2056 + 8 * D].rearrange("p (k c) -> p k c", k=8); wq_sb.R = R("wq_sb")
            NX = 2
            xt = [sb("xt%d" % i, [128, D], F32, stack=sa) for i in range(NX)]
            ss = [sb("ss%d" % i, [128, 2], F32, stack=sa) for i in range(2)]
            xn = [sb("xn%d" % i, [128, D], BF16, stack=sa) for i in range(2)]
            uT = [sb("uT%d" % i, [128, 8, 512], BF16, n=4, stack=sa) for i in range(2)]
            tb = [sb("tb%d" % i, [128, 4, 64], F32, stack=sa) for i in range(2)]
            ktok = [sb("ktok%d" % i, [128, H, 128], BF16, n=4, stack=sa) for i in range(2)]
            rtmp = [sb("rtmp%d" % i, [128, H, 64], F32, n=6, stack=sa) for i in range(2)]
            KTc = [sb("KTc0", [128, H, 512], BF16, n=4, stack=sa)] * 2
            Vc = [sb("Vc%d" % i, [128, H, 4, VW], BF16, n=2, stack=sa) for i in range(2)]
            kps = ps("kps", [128, 2 * 512], F32, n=2, stack=sa)
            vps = ps("vps", [128, 2 * 512], F32, n=2, stack=sa)
            ktp = [ps("ktp%d" % i, [128, H, 128], BF16, stack=sa) for i in range(2)]

            for kt in range(8):
                P.op("gpsimd", lambda e, kt=kt: e.dma_start(out=wkv_sb.t[:, kt, :], in_=wkv[kt * 128:(kt + 1) * 128, :]),
                     writes=[wkv_sb.R], dma=1)
            for kt in range(8):
                P.op("gpsimd", lambda e, kt=kt: e.dma_start(out=wq_sb.t[:, kt, :], in_=wq[kt * 128:(kt + 1) * 128, :]),
                     writes=[wq_sb.R], dma=1)
            for v in Vc:
                P.op("gpsimd", lambda e, v=v: e.memset(v.t[:, :, :, 128:VW], 1.0), writes=[v.R])

            tix = 0
            Rkt_chunks = []; Rv_chunks = []
            _lim = int(os.environ.get('KLIM', NCH)); _part = int(os.environ.get('KPART', 9)); _sub = int(os.environ.get('KSUB', 9))
            def chunk_vars(c):
                return (c % 4 == 3), c // 4, c % 2

            def norm_tile(c, t):
                nonlocal tix
                own, og, cb = chunk_vars(c)
                if t == 0:
                    P.op("sync", lambda e, c=c, cb=cb: e.dma_start(out=tb[cb].t[:], in_=tabK[:, 4 * c:4 * c + 4, :]),
                         writes=[tb[cb].R], dma=1)
                for t in (t,):
                    xi = tix % NX
                    sb2 = tix % 2
                    row0 = c * 512 + t * 128
                    P.op("sync", lambda e, xi=xi, row0=row0: e.dma_start(out=xt[xi].t[:], in_=xb[row0:row0 + 128, :]),
                         writes=[xt[xi].R], dma=1)
                    P.op("scalar", lambda e, xi=xi, sb2=sb2: e.activation(out=xn[sb2].t[:], in_=xt[xi].t[:], func=AF.Square,
                                                                         accum_out=ss[sb2].t[:, 0:1]),
                         reads=[xt[xi].R], writes=[xn[sb2].R, ss[sb2].R])
                    P.op("scalar", lambda e, sb2=sb2: e.activation(out=ss[sb2].t[:, 1:2], in_=ss[sb2].t[:, 0:1], func=AF.Sqrt,
                                                                   bias=EPS, scale=1.0 / D),
                         reads=[ss[sb2].R], writes=[ss[sb2].R])
                    P.op("vector", lambda e, sb2=sb2: e.reciprocal(out=ss[sb2].t[:, 1:2], in_=ss[sb2].t[:, 1:2]),
                         reads=[ss[sb2].R], writes=[ss[sb2].R])
                    P.op("vector", lambda e, xi=xi, sb2=sb2: e.tensor_scalar(out=xn[sb2].t[:], in0=xt[xi].t[:],
                                                                            scalar1=ss[sb2].t[:, 1:2], scalar2=None, op0=ALU.mult),
                         reads=[xt[xi].R, ss[sb2].R], writes=[xn[sb2].R])
                    tix += 1
                    kb = tix % 2
                    def part_b(sb2=sb2, kb=kb, cb=cb, t=t, own=own, og=og):
                        def tr8(e, sb2=sb2, kb=kb):
                            ins = None
                            for kt in range(8):
                                ins = e.transpose(out=ktp[kb].t[:, kt, :], in_=xn[sb2].t[:, kt * 128:(kt + 1) * 128],
                                                  identity=ident.t[:])
                            return ins
                        P.op("tensor", tr8, reads=[xn[sb2].R, ident.R], writes=[ktp[kb].R])
                        def ev8(e, kb=kb, cb=cb, t=t):
                            return e.tensor_tensor(out=uT[cb].t[:, :, t * 128:(t + 1) * 128], in0=ktp[kb].t[:],
                                                   in1=gt.t[:, 0:8].unsqueeze(2).to_broadcast([128, 8, 128]), op=ALU.mult)
                        P.op("vector", ev8, reads=[gt.R], writes=[uT[cb].Rs[t], ktp[kb].R])
                        if own:
                            P.op("gpsimd", lambda e, cb=cb, t=t, og=og: e.tensor_copy(
                                out=uTo.t[:, :, og * 512 + t * 128: og * 512 + (t + 1) * 128],
                                in_=uT[cb].t[:, :, t * 128:(t + 1) * 128]),
                                reads=[uT[cb].Rs[t]], writes=[uTo.Rs[og]])
                return part_b
            norm_b_pending = {}
            def proj_tile(c, t, mid=None):
                own, og, cb = chunk_vars(c)
                for t in (t,):
                    tt = c * 4 + t
                    kb = tt % 2
                    sl = slice(t * 128, (t + 1) * 128)
                    plist = [("k", kps, wkv_sb, 0), ("v", vps, wkv_sb, D)]
                    if own:
                        plist.append(("q", kps, wq_sb, 0))
                    deferred = []
                    for (nm, pst, wsb, coff) in plist:
                        if nm == "q":
                            if mid is not None:
                                mid()
                                mid = None
                            for fn_ in deferred:
                                fn_()
                            deferred = []
                        for hb in range(2):
                            def mm(e, pst=pst, wsb=wsb, coff=coff, sl=sl, cb=cb, hb=hb):
                                ins = None
                                for kt in range(8):
                                    ins = e.matmul(pst.t[:, hb * 512:(hb + 1) * 512], lhsT=uT[cb].t[:, kt, sl],
                                                   rhs=wsb.t[:, kt, coff + hb * 512: coff + (hb + 1) * 512],
                                                   start=(kt == 0), stop=(kt == 7))
                                return ins
                            P.op("tensor", mm, reads=[uT[cb].Rs[t], wsb.R], writes=[pst.Rs[hb]])
                            hs = slice(4 * hb, 4 * hb + 4)
                            pv = pst.t[:, hb * 512:(hb + 1) * 512].rearrange("p (h d) -> p h d", d=128)
                            if _sub < 2:
                                continue
                            if nm == "v":
                                P.op("scalar", lambda e, cb=cb, t=t, hs=hs, pv=pv: e.activation(
                                    out=Vc[cb].t[:, hs, t, 0:128], in_=pv, func=AF.Copy),
                                    writes=[Vc[cb].Rs[hb], pst.Rs[hb]])
                                continue
                            if _sub < 3:
                                continue
                            tbl = tb[cb].t
                            def r_a(e, pv=pv, kb=kb, t=t, tbl=tbl, hs=hs):
                                return e.tensor_tensor(out=rtmp[kb].t[:, hs, 0:32], in0=pv[:, :, 0:32],
                                                       in1=tbl[:, t, 0:32].unsqueeze(1).to_broadcast([128, 4, 32]), op=ALU.mult)
                            def r_b1(e, pv=pv, kb=kb, t=t, tbl=tbl, hs=hs):
                                return e.tensor_tensor(out=rtmp[kb].t[:, hs, 32:48], in0=pv[:, :, 16:32],
                                                       in1=tbl[:, t, 32:48].unsqueeze(1).to_broadcast([128, 4, 16]), op=ALU.mult)
                            def r_b2(e, pv=pv, kb=kb, t=t, tbl=tbl, hs=hs):
                                return e.tensor_tensor(out=rtmp[kb].t[:, hs, 48:64], in0=pv[:, :, 0:16],
                                                       in1=tbl[:, t, 48:64].unsqueeze(1).to_broadcast([128, 4, 16]), op=ALU.mult)
                            P.op("vector", r_a, reads=[tb[cb].R], writes=[rtmp[kb].Rs[3 * hb + 0], pst.Rs[hb]])
                            P.op("vector", r_b1, reads=[tb[cb].R], writes=[rtmp[kb].Rs[3 * hb + 1], pst.Rs[hb]])
                            P.op("vector", r_b2, reads=[tb[cb].R], writes=[rtmp[kb].Rs[3 * hb + 2], pst.Rs[hb]])
                            P.op("vector", lambda e, kb=kb, hs=hs: e.tensor_tensor(out=ktok[kb].t[:, hs, 0:32], in0=rtmp[kb].t[:, hs, 0:32],
                                                                                    in1=rtmp[kb].t[:, hs, 32:64], op=ALU.add),
                                 reads=rtmp[kb].Rs[3 * hb:3 * hb + 3], writes=[ktok[kb].Rs[hb]])
                            if _sub < 4:
                                continue
                            P.op("scalar", lambda e, pv=pv, kb=kb, hs=hs: e.activation(out=ktok[kb].t[:, hs, 32:128], in_=pv[:, :, 32:128],
                                                                                       func=AF.Copy),
                                 writes=[ktok[kb].Rs[2 + hb], pst.Rs[hb]])
                        if nm == "v" or _sub < 5:
                            continue
                        def post(nm=nm, kb=kb, cb=cb, sl=sl, t=t, tt=tt, c=c, og=og):
                            def tr(e):
                                ins = None
                                for h in range(H):
                                    ins = e.transpose(out=ktp[kb].t[:, h, :], in_=ktok[kb].t[:, h, :], identity=ident.t[:])
                                return ins
                            P.op("tensor", tr, reads=ktok[kb].Rs + [ident.R], writes=[ktp[kb].R])
                            if nm == "k":
                                P.op("scalar", lambda e: e.activation(out=KTc[cb].t[:, :, sl], in_=ktp[kb].t[:], func=AF.Copy),
                                     writes=[KTc[cb].Rs[t], ktp[kb].R])
                                P.op("vector", lambda e: e.tensor_reduce(out=kmeanF.t[:, :, tt:tt + 1], in_=KTc[cb].t[:, :, sl], axis=AX.X, op=ALU.add),
                                     reads=[KTc[cb].Rs[t]], writes=[kmeanF.R])
                                Rk_ = R("ktscr%d" % tt)
                                Rkt_chunks.append(Rk_)
                                P.op("gpsimd", lambda e: e.dma_start(out=kt_scr[:, :, c * 512 + t * 128:c * 512 + (t + 1) * 128], in_=KTc[cb].t[:, :, sl]),
                                     reads=[KTc[cb].Rs[t]], writes=[Rk_], dma=1, semkey=("d", "ktscr", t))
                            else:
                                osl = slice(og * 512 + t * 128, og * 512 + (t + 1) * 128)
                                P.op("scalar", lambda e: e.activation(out=QT.t[:, :, osl], in_=ktp[kb].t[:], func=AF.Copy),
                                     writes=[QT.Rs[og], ktp[kb].R])
                        deferred.append(post)
                    if mid is not None:
                        mid()
                    for fn_ in deferred:
                        fn_()
            def epilogue(c):
                own, og, cb = chunk_vars(c)
                P.op("gpsimd", lambda e, cb=cb, c=c: e.dma_start(out=v_scr[:, :, 4 * c:4 * c + 4, :], in_=Vc[cb].t[:]),
                     reads=Vc[cb].Rs + [Vc[cb].R], writes=[Rv_chunks.append(R("vscr%d" % c)) or Rv_chunks[-1]], dma=1, semkey=("d", "vscr", cb))
            NT = 4 * _lim
            LEAD = 2
            for g_ in range(min(LEAD, NT)):
                norm_tile(g_ // 4, g_ % 4)()
            for g_ in range(NT):
                pb_ = None
                if g_ + LEAD < NT:
                    pb_ = norm_tile((g_ + LEAD) // 4, (g_ + LEAD) % 4)
                proj_tile(g_ // 4, g_ % 4, pb_)
                if g_ % 4 == 3:
                    epilogue(g_ // 4)
            kmv = kmeanF.t[:].rearrange("p h (b two) -> p h b two", two=2)
            P.op("vector", lambda e: e.tensor_tensor(out=kmv[:, :, :, 0], in0=kmv[:, :, :, 0], in1=kmv[:, :, :, 1], op=ALU.add),
                 reads=[kmeanF.R], writes=[kmeanF.R])
            P.op("vector", lambda e: e.tensor_scalar(out=kmeanT.t[:], in0=kmv[:, :, :, 0], scalar1=1.0 / 256, scalar2=None, op0=ALU.mult),
                 reads=[kmeanF.R], writes=[kmeanT.R])
            if stage == "A":
                Ro = [R("o_qt"), R("o_km")]
                P.op("sync", lambda e: e.dma_start(out=o_qt[:, :, :], in_=QT.t[:]), reads=QT.Rs, writes=[Ro[0]], dma=1)
                P.op("sync", lambda e: e.dma_start(out=o_km[:, :, :], in_=kmeanF.t[:]), reads=[kmeanF.R], writes=[Ro[1]], dma=1)
                outs += [r.w for r in Ro]

        P.barrier()
        if stage != "A":
          with ExitStack() as sbk:
            mt = sb("mt", [128, 3, 16, 32], F32, stack=sbk)
            cstb = sb("cstb", [128, 256], BF16, stack=sbk)
            eo = sb("eo", [128, 32, 128], BF16, stack=sbk)
            KTh = [sb("KTh%d" % i, [128, S], BF16, stack=sbk) for i in range(2)]
            Vh = [sb("Vh%d" % i, [128, 64, VW], BF16, stack=sbk) for i in range(2)]
            gsb = sb("gsb", [128, 4, 32], F32, stack=sbk)
            mx8 = sb("mx8", [128, 4, 8], F32, stack=sbk)
            sbias = sb("sbias", [128, 4, 32], F32, stack=sbk)
            nmb = sb("nmb", [128, 4, 32], BF16, stack=sbk)
            nmT = [sb("nmT%d" % i, [128, 512], BF16, stack=sbk) for i in range(2)]
            NPT = 4
            PT = [sb("PT%d" % i, [128, 2, 512], BF16, n=2, stack=sbk) for i in range(NPT)]
            rec = sb("rec", [128, 4], F32, n=2, stack=sbk)
            atok = sb("atok", [128, 4, 128], BF16, n=4, stack=sbk)
            stp = [ps("stp%d" % i, [128, 2, 512], F32, n=2, stack=sbk) for i in range(2)]
            ob = [ps("ob%d" % i, [128, 512], F32, stack=sbk) for i in range(2)]
            gps = ps("gps", [128, 512], F32, stack=sbk)
            tps = ps("tps", [128, 2, 512], BF16, stack=sbk)

            P.op("sync", lambda e: e.dma_start(out=mt.t[:], in_=mtab[:, :, :, :]), writes=[mt.R], dma=1)
            P.op("gpsimd", lambda e: e.dma_start(out=cstb.t[:], in_=cst[:, :]), writes=[cstb.R], dma=1)
            P.op("gpsimd", lambda e: e.dma_start(out=eo.t[:], in_=eoh[:, :, :]), writes=[eo.R], dma=1)
            for q_ in range(2):
                P.op("vector", lambda e, q_=q_: e.memset(nmT[q_].t[:], 0.0), writes=[nmT[q_].R])

            def load_head(h):
                hb2 = h % 2
                P.op("sync", lambda e: e.dma_start(out=KTh[hb2].t[:], in_=kt_scr[:, h, :]), reads=Rkt_chunks, writes=[KTh[hb2].R], dma=1)
                P.op("sync", lambda e: e.dma_start(out=Vh[hb2].t[:], in_=v_scr[:, h, :, :]), reads=Rv_chunks, writes=[Vh[hb2].R], dma=1)

            _nh = int(os.environ.get('KHEADS', H))
            load_head(0)
            blkctr = 0
            def mask1(h, i):
                def gate(e):
                    ins = None
                    for qt in range(4):
                        ins = e.matmul(gps.t[:, qt * 32:(qt + 1) * 32], lhsT=QT.t[:, h, 512 * i + 128 * qt: 512 * i + 128 * (qt + 1)],
                                       rhs=kmeanT.t[:, h, :], start=True, stop=True)
                    return ins
                P.op("tensor", gate, reads=[QT.Rs[i], kmeanT.R], writes=[gps.R])
                P.op("vector", lambda e: e.tensor_tensor(out=gsb.t[:], in0=gps.t[:, 0:128].rearrange("p (a n) -> p a n", n=32),
                                                         in1=mt.t[:, 0, 4 * i:4 * i + 4, :], op=ALU.add),
                     reads=[mt.R], writes=[gsb.R, gps.R])
                for qt in range(4):
                    P.op("vector", lambda e, qt=qt: e.max(out=mx8.t[:, qt, :], in_=gsb.t[:, qt, :]), reads=[gsb.R], writes=[mx8.R])
                for qt in range(4):
                    P.op("vector", lambda e, qt=qt: e.tensor_scalar(out=sbias.t[:, qt, :], in0=gsb.t[:, qt, :], scalar1=mx8.t[:, qt, 2:3],
                                                                    scalar2=-BIG, op0=ALU.is_lt, op1=ALU.mult),
                         reads=[gsb.R, mx8.R], writes=[sbias.R])
                P.op("vector", lambda e: e.tensor_tensor(out=sbias.t[:], in0=sbias.t[:], in1=mt.t[:, 1, 4 * i:4 * i + 4, :], op=ALU.mult),
                     reads=[sbias.R, mt.R], writes=[sbias.R])
                P.op("vector", lambda e: e.tensor_tensor(out=nmb.t[:], in0=sbias.t[:], in1=mt.t[:, 2, 4 * i:4 * i + 4, :], op=ALU.add),
                     reads=[sbias.R, mt.R], writes=[nmb.R])

            def mask2(mb):
                def trm(e):
                    ins = None
                    for qt in range(4):
                        ins = e.transpose(out=tps.t[0:32, 0, qt * 128:(qt + 1) * 128], in_=nmb.t[:, qt, :], identity=ident.t[:])
                    return ins
                P.op("tensor", trm, reads=[nmb.R, ident.R], writes=[tps.R])
                P.op("scalar", lambda e: e.activation(out=nmT[mb].t[0:32, :], in_=tps.t[0:32, 0, :], func=AF.Copy),
                     writes=[nmT[mb].R, tps.R])

            def fin1():
                for b2 in range(2):
                    P.op("vector", lambda e, b2=b2: e.reciprocal(out=rec.t[:, 2 * b2:2 * b2 + 2],
                                                                 in_=ob[b2].t[:].rearrange("p (c w) -> p c w", w=256)[:, :, 128]),
                         writes=[rec.Rs[b2], ob[b2].R])
                for c in range(4):
                    P.op("vector", lambda e, c=c: e.tensor_scalar(out=atok.t[:, c, :], in0=ob[c // 2].t[:, (c % 2) * 256:(c % 2) * 256 + 128],
                                                                  scalar1=rec.t[:, c:c + 1], scalar2=None, op0=ALU.mult),
                         reads=[rec.Rs[c // 2]], writes=[atok.Rs[c], ob[c // 2].R])

            def fin2(h, i):
                def tra(e):
                    ins = None
                    for c in range(4):
                        ins = e.transpose(out=tps.t[:, 1, c * 128:(c + 1) * 128], in_=atok.t[:, c, :], identity=ident.t[:])
                    return ins
                P.op("tensor", tra, reads=atok.Rs + [ident.R], writes=[tps.R])
                P.op("scalar", lambda e: e.activation(out=attnT.t[:, h, 512 * i:512 * i + 512], in_=tps.t[:, 1, :], func=AF.Copy),
                     writes=[attnT.Rs[i], tps.R])

            groups = [(h, i) for h in range(_nh) for i in range(int(os.environ.get('KGROUPS', 4)))]
            mask1(*groups[0])
            mask2(0)
            for gi, (h, i) in enumerate(groups):
                if i == 0 and h + 1 < _nh:
                    load_head(h + 1)
                hb2 = h % 2
                if True:
                    qsl = slice(512 * i, 512 * i + 512)
                    nb = 8 * i + 8
                    mb = gi % 2
                    def emit_qk(n, sbuf, h=h, i=i, qsl=qsl, hb2=hb2, mb=mb):
                        def f(e):
                            ins = None
                            for a in range(2):
                                kt = 2 * n + a
                                own = n >= 8 * i + 6
                                c0 = 256 if n == 8 * i + 7 else 0
                                dst = stp[sbuf].t[:, a, c0:512]
                                e.matmul(dst, lhsT=KTh[hb2].t[:, kt * 128:(kt + 1) * 128], rhs=QT.t[:, h, 512 * i + c0:512 * i + 512], start=True, stop=False)
                                ins = e.matmul(dst, lhsT=eo.t[:, n, :], rhs=nmT[mb].t[:, c0:512], start=False, stop=(not own))
                                if own:
                                    ar = 2 * (n - (8 * i + 6)) + a
                                    if ar % 2 == 1:
                                        e.matmul(stp[sbuf].t[:, a, (ar - 1) * 128: ar * 128], lhsT=ident.t[:], rhs=cstb.t[:, 128:256],
                                                 start=False, stop=False)
                                    ins = e.matmul(stp[sbuf].t[:, a, ar * 128:(ar + 1) * 128], lhsT=ident.t[:], rhs=cstb.t[:, 0:128],
                                                   start=False, stop=True)
                            return ins
                        P.op("tensor", f, reads=[KTh[hb2].R, QT.Rs[i], eo.R, nmT[mb].R, ident.R, cstb.R], writes=[stp[sbuf].Rs[0], stp[sbuf].Rs[1]])

                    def emit_exp(n, sbuf, pb, i=i):
                        c0 = 256 if n == 8 * i + 7 else 0
                        for a in range(2):
                            P.op("scalar", lambda e, a=a: e.activation(out=PT[pb].t[:, a, c0:512], in_=stp[sbuf].t[:, a, c0:512], func=AF.Exp, scale=SCALE),
                                 writes=[PT[pb].Rs[a], stp[sbuf].Rs[a]])

                    def emit_pv(n, pb, h=h, i=i, hb2=hb2, nb=nb):
                        def f(e):
                            ins = None
                            for a in range(2):
                                kt = 2 * n + a
                                ar = 2 * (n - (8 * i + 6)) + a if n >= 8 * i + 6 else -1
                                for c in range(4):
                                    if c < ar:
                                        continue
                                    first = (n == 0 and a == 0)
                                    last = (n == nb - 1 and a == 1) or (n == nb - 1 and a == 0 and c < 2 * (n - (8 * i + 6)) + 1)
                                    ins = e.matmul(ob[c // 2].t[:, (c % 2) * 256:(c % 2) * 256 + 129], lhsT=PT[pb].t[:, a, c * 128:(c + 1) * 128],
                                                   rhs=Vh[hb2].t[:, kt, 0:129], start=(first and c % 2 == 0), stop=last, skip_group_check=True)
                            return ins
                        P.op("tensor", f, reads=[PT[pb].Rs[0], PT[pb].Rs[1], Vh[hb2].R], writes=[ob[0].R, ob[1].R])

                    pend = []
                    for n in range(nb):
                        sbuf = blkctr % 2
                        pb = blkctr % NPT
                        blkctr += 1
                        emit_qk(n, sbuf)
                        emit_exp(n, sbuf, pb)
                        pend.append((n, pb))
                        if len(pend) > 2:
                            emit_pv(*pend.pop(0))
                        if n == 0 and gi + 1 < len(groups):
                            mask1(*groups[gi + 1])
                        if n == 2 and gi >= 1:
                            fin2(*groups[gi - 1])
                        if n == 4 and gi + 1 < len(groups):
                            mask2((gi + 1) % 2)
                    while pend:
                        emit_pv(*pend.pop(0))
                    fin1()
            fin2(*groups[-1])
            if stage == "B":
                Rq = R("o_qt2")
                P.op("sync", lambda e: e.dma_start(out=o_qt2[:, :, :], in_=QT.t[:]), reads=QT.Rs, writes=[Rq], dma=1)
                outs.append(Rq.w)
                Rd = [R("dbg%d" % q) for q in range(3)]
                P.op("sync", lambda e: e.dma_start(out=o_d1[:, :, :], in_=gsb.t[:]), reads=[gsb.R], writes=[Rd[0]], dma=1)
                P.op("sync", lambda e: e.dma_start(out=o_d2[:, :, :], in_=nmb.t[:]), reads=[nmb.R], writes=[Rd[1]], dma=1)
                P.op("sync", lambda e: e.dma_start(out=o_d3[:, :], in_=nmT[0].t[0:32, :]), reads=[nmT[0].R], writes=[Rd[2]], dma=1)
                outs += [r.w for r in Rd]
                Ro = R("o_at")
                P.op("sync", lambda e: e.dma_start(out=o_at[:, :, :], in_=attnT.t[:]), reads=attnT.Rs, writes=[Ro], dma=1)
                outs.append(Ro.w)

        if stage == "full":
          P.barrier()
          with ExitStack() as sc:
            M1 = sb("M1", [128, 8 * 4 * 514], BF16, stack=sc)
            M2 = sb("M2", [128, 8 * 2048], BF16, stack=sc)
            z = M1.t[:, :].rearrange("p (f g t) -> p f g t", f=8, g=4)
            mrg = M1.t[:, 0:8 * 2048].rearrange("p (f t) -> p f t", f=8)
            zc = M2.t[:, :].rearrange("p (f t) -> p f t", f=8)
            zR = [R("z%d" % q) for q in range(8)]; zcR = [R("zc%d" % q) for q in range(8)]; mR = [R("mrg%d" % q) for q in range(8)]
            hview = UT.t[:, 0:2 * 16 * D].bitcast(F32).rearrange("p (t d) -> p t d", t=16)
            hR = [R("h%d" % q) for q in range(16)]
            wdn = QTW.t[:, :].rearrange("p (k c) -> p k c", k=22); wdnR = R("wdn")
            NW = 2
            WBC = 256
            wb = [sb("wb%d" % q, [128, 8, WBC], BF16, stack=sc) for q in range(NW)]
            NWX = 5
            wbx = []
            for q_ in range(NWX):
                v_ = _V(); v_.t = QTW.t[:, q_ * 8 * WBC:(q_ + 1) * 8 * WBC].rearrange("p (k c) -> p k c", k=8); v_.R = R("wbx%d" % q_); v_.name = "wbx%d" % q_
                wbx.append(v_)
            wring = wb + wbx
            gf = sb("gf", [128, D], F32, stack=sc)
            ctmp = sb("ctmp", [128, 512], F32, stack=sc)
            sg = [sb("sg%d" % q, [128, 512], F32, stack=sc) for q in range(2)]
            xo = [sb("xo%d" % q, [128, 256], F32, stack=sc) for q in range(2)]
            xhn = sb("xhn", [8, D], BF16, stack=sc)
            st8 = sb("st8", [8, 2], F32, stack=sc)
            ssc = [sb("ssc%d" % q, [128, 2], F32, stack=sc) for q in range(2)]
            xnc = [sb("xnc0", [128, D], BF16, stack=sc)] * 2
            ot = [sb("ot0", [128, D], F32, stack=sc)] * 2
            xhs = _V(); xhs.t = ot[0].t[0:8, :]; xhs.R = ot[0].R
            xhq = xhn
            xnc2 = _V(); xnc2.t = ctmp.t[:, :].bitcast(BF16); xnc2.R = R("xnc_alt")
            xnc = [xnc[0], xnc2]
            NPS = 6
            pp = [ps("pp%d" % q, [128, 512], F32, stack=sc) for q in range(NPS)]
            tpc = [ps("tpc%d" % q, [128, 8, 128], BF16, stack=sc) for q in range(2)]
            cnt = {"w": 0, "p": 0, "x": 0, "s": 0}

            def nxt(k, n):
                v = cnt[k] % n
                cnt[k] += 1
                return v

            def stream(wd, col0, ncols=256, dst0=0, buf=None):
                if buf is None:
                    buf = wring[nxt("w", len(wring))]
                P.op("gpsimd", lambda e: e.dma_start(out=buf.t[:, :, dst0:dst0 + ncols],
                                                     in_=wd[:, col0:col0 + ncols].rearrange("(k p) c -> p k c", p=128)),
                     writes=[buf.R], dma=1)
                return buf

            def proj(buf, c0, act, actR, sl, n=512):
                p = pp[nxt("p", NPS)]
                def f(e):
                    ins = None
                    for kt in range(8):
                        ins = e.matmul(p.t[:, 0:n], lhsT=buf.t[:, kt, c0:c0 + 128], rhs=act[:, kt, sl], start=(kt == 0), stop=(kt == 7))
                    return ins
                P.op("tensor", f, reads=[buf.R] + list(actR), writes=[p.R])
                return p

            P.op("sync", lambda e: e.dma_start(out=gf.t[:], in_=gfin[:, :]), writes=[gf.R], dma=1)
            P.op("sync", lambda e: e.dma_start(out=xhs.t[:], in_=xh[:, :]), writes=[xhs.R], dma=1)
            P.op("scalar", lambda e: e.activation(out=xhq.t[:], in_=xhs.t[:], func=AF.Square, accum_out=st8.t[:, 0:1]), reads=[xhs.R], writes=[xhq.R, st8.R])
            P.op("scalar", lambda e: e.activation(out=st8.t[:, 1:2], in_=st8.t[:, 0:1], func=AF.Sqrt, bias=EPS, scale=1.0 / D), reads=[st8.R], writes=[st8.R])
            P.op("vector", lambda e: e.reciprocal(out=st8.t[:, 1:2], in_=st8.t[:, 1:2]), reads=[st8.R], writes=[st8.R])
            P.op("vector", lambda e: e.tensor_scalar(out=xhn.t[:], in0=xhs.t[:], scalar1=st8.t[:, 1:2], scalar2=None, op0=ALU.mult), reads=[xhs.R, st8.R], writes=[xhn.R])
            def trh(e):
                ins = None
                for kt in range(8):
                    ins = e.transpose(out=tpc[0].t[:, kt, 0:8], in_=xhn.t[:, kt * 128:(kt + 1) * 128], identity=ident.t[0:8, 0:8])
                return ins
            P.op("tensor", trh, reads=[xhn.R, ident.R], writes=[tpc[0].R])
            P.op("vector", lambda e: e.tensor_tensor(out=uTo.t[:, :, 2048:2056], in0=tpc[0].t[:, :, 0:8],
                                                     in1=gt.t[:, 0:8].unsqueeze(2).to_broadcast([128, 8, 8]), op=ALU.mult),
                 reads=[gt.R], writes=[uTo.Rs[4], tpc[0].R])

            chunks = [(slice(512 * g, 512 * g + 512), 512, g) for g in range(4)] + [(slice(2048, 2056), 8, 4)]
            def c1_proj(pname, col, cbk):
                buf = stream(wrest, col + cbk * 256)
                for f4 in range(2):
                    ft = cbk * 2 + f4
                    for (sl, n, g) in (chunks if pname != "bg" else chunks[:4]):
                        p = proj(buf, f4 * 128, uTo.t, [uTo.Rs[g]], sl, n)
                        if pname == "bg":
                            P.op("vector", lambda e, p=p, ft=ft, sl=sl: e.tensor_tensor(out=zc[:, ft, sl], in0=p.t[:], in1=zc[:, ft, sl], op=ALU.mult),
                                 reads=[zcR[ft]], writes=[zcR[ft], p.R])
                            continue
                        dst = z[:, ft, g, 2:514] if g < 4 else z[:, ft, :, 0:2]
                        src = p.t[:, 0:n] if g < 4 else p.t[:, 0:8].rearrange("p (g t) -> p g t", t=2)
                        if pname == "xc":
                            P.op("scalar", lambda e, dst=dst, src=src: e.activation(out=dst, in_=src, func=AF.Copy), writes=[zR[ft], p.R])
                        else:
                            P.op("vector", lambda e, dst=dst, src=src: e.tensor_tensor(out=dst, in0=src, in1=dst, op=ALU.mult), writes=[zR[ft], p.R])

            def conv_ft(ft):
                for g in range(4):
                    zt = z[:, ft, g, :]
                    zo = zc[:, ft, 512 * g:512 * g + 512]
                    P.op("vector", lambda e, zt=zt: e.tensor_scalar(out=ctmp.t[:], in0=zt[:, 2:514], scalar1=gt.t[:, 32 + ft:33 + ft], scalar2=None, op0=ALU.mult),
                         reads=[zR[ft], gt.R], writes=[ctmp.R])
                    P.op("vector", lambda e, zt=zt: e.scalar_tensor_tensor(out=ctmp.t[:], in0=zt[:, 1:513], scalar=gt.t[:, 24 + ft:25 + ft], in1=ctmp.t[:], op0=ALU.mult, op1=ALU.add),
                         reads=[zR[ft], gt.R, ctmp.R], writes=[ctmp.R])
                    P.op("vector", lambda e, zt=zt, zo=zo: e.scalar_tensor_tensor(out=zo, in0=zt[:, 0:512], scalar=gt.t[:, 16 + ft:17 + ft], in1=ctmp.t[:], op0=ALU.mult, op1=ALU.add),
                         reads=[zR[ft], gt.R, ctmp.R], writes=[zcR[ft]])

            for cbk in range(4):
                c1_proj("xc", 2 * D, cbk)
            for cbk in range(4):
                c1_proj("cg", 0, cbk)
                conv_ft(2 * cbk)
                conv_ft(2 * cbk + 1)
                if cbk >= 1:
                    c1_proj("bg", D, cbk - 1)
            c1_proj("bg", D, 3)
            for (gcol, wd, act, actRs, first) in ((4 * D, w_cb, zc, zcR, True), (3 * D, w_ab, attnT.t, None, False)):
                for cbk in range(8):
                    bg = wring[nxt("w", len(wring))]
                    stream(wrest, gcol + cbk * 128, 128, 0, bg)
                    stream(wd, cbk * 128, 128, 128, bg)
                    bw = bg
                    for f4 in range(1):
                        dtile = cbk
                        for (sl, n, g) in chunks[:4]:
                            pg = proj(bg, 0, uTo.t, [uTo.Rs[g]], sl)
                            py = proj(bw, 128, act, (actRs if actRs is not None else [attnT.Rs[g]]), sl)
                            s_ = sg[nxt("s", 2)]
                            P.op("scalar", lambda e, pg=pg, s_=s_: e.activation(out=s_.t[:], in_=pg.t[:], func=AF.Sigmoid), writes=[s_.R, pg.R])
                            if first:
                                P.op("vector", lambda e, py=py, s_=s_, dtile=dtile, sl=sl: e.tensor_tensor(out=mrg[:, dtile, sl], in0=py.t[:], in1=s_.t[:], op=ALU.mult),
                                     reads=[s_.R] + zR, writes=[mR[dtile], py.R])
                            else:
                                P.op("vector", lambda e, py=py, s_=s_: e.tensor_tensor(out=s_.t[:], in0=py.t[:], in1=s_.t[:], op=ALU.mult),
                                     reads=[s_.R], writes=[s_.R, py.R])
                                P.op("vector", lambda e, s_=s_, dtile=dtile, sl=sl: e.tensor_tensor(out=mrg[:, dtile, sl], in0=mrg[:, dtile, sl], in1=s_.t[:], op=ALU.add),
                                     reads=[s_.R], writes=[mR[dtile]])
            P.barrier()
            for cbk in range(4):
                buf = stream(w_o, cbk * 256)
                for tt in range(16):
                    row0 = (4 * (tt // 4) + 3) * 512 + (tt % 4) * 128
                    xb_ = xo[nxt("x", 2)]
                    P.op("sync", lambda e, xb_=xb_, row0=row0, cbk=cbk: e.dma_start(out=xb_.t[:], in_=xb[row0:row0 + 128, cbk * 256:(cbk + 1) * 256]),
                         writes=[xb_.R], dma=1)
                    p = pp[nxt("p", NPS)]
                    def f(e, p=p, tt=tt, buf=buf):
                        ins = None
                        for kt in range(8):
                            ins = e.matmul(p.t[:, 0:256], lhsT=mrg[:, kt, tt * 128:(tt + 1) * 128], rhs=buf.t[:, kt, :], start=(kt == 0), stop=(kt == 7))
                        return ins
                    P.op("tensor", f, reads=[buf.R] + mR, writes=[p.R])
                    P.op("vector", lambda e, p=p, xb_=xb_, tt=tt, cbk=cbk: e.tensor_tensor(out=hview[:, tt, cbk * 256:(cbk + 1) * 256], in0=p.t[:, 0:256], in1=xb_.t[:], op=ALU.add),
                         reads=[xb_.R], writes=[hR[tt], p.R])
            P.barrier()
            u2T = M1.t[:, 0:8 * 1024].rearrange("p (k t) -> p k t", k=8); u2R = [R("u2T%d" % q) for q in range(8)]
            MM = M2.t[:, :]
            aT_a = M2.t[:, 0:16 * 1024].rearrange("p (f t) -> p f t", f=16)
            aT_b = M1.t[:, 8 * 1024:14 * 1024].rearrange("p (f t) -> p f t", f=6)
            aR = [R("aT%d" % q) for q in range(22)]
            def aT(f):
                return aT_a[:, f, :] if f < 16 else aT_b[:, f - 16, :]
            w3 = _V(); w3.t = M1.t[:, 14 * 1024:14 * 1024 + 8 * WBC].rearrange("p (k c) -> p k c", k=8); w3.R = R("wb_m1"); w3.name = "wb_m1"
            wff = wb + [w3]

            def rms(tt, srcR):
                s2 = ssc[tt % 2]
                sqc = xnc[tt % 2]
                P.op("scalar", lambda e: e.activation(out=sqc.t[:], in_=hview[:, tt, :], func=AF.Square, accum_out=s2.t[:, 0:1]), reads=[srcR], writes=[sqc.R, s2.R])
                P.op("scalar", lambda e: e.activation(out=s2.t[:, 1:2], in_=s2.t[:, 0:1], func=AF.Sqrt, bias=EPS, scale=1.0 / D), reads=[s2.R], writes=[s2.R])
                P.op("vector", lambda e: e.reciprocal(out=s2.t[:, 1:2], in_=s2.t[:, 1:2]), reads=[s2.R], writes=[s2.R])
                return s2

            for hf in range(2):
                for t8 in range(8):
                    tt = hf * 8 + t8
                    s2 = rms(tt, hR[tt])
                    xn_ = xnc[tt % 2]
                    P.op("vector", lambda e, tt=tt, s2=s2, xn_=xn_: e.tensor_scalar(out=xn_.t[:], in0=hview[:, tt, :], scalar1=s2.t[:, 1:2], scalar2=None, op0=ALU.mult),
                         reads=[hR[tt], s2.R], writes=[xn_.R])
                    tp_ = tpc[tt % 2]
                    def tr8(e, xn_=xn_, tp_=tp_):
                        ins = None
                        for kt in range(8):
                            ins = e.transpose(out=tp_.t[:, kt, :], in_=xn_.t[:, kt * 128:(kt + 1) * 128], identity=ident.t[:])
                        return ins
                    P.op("tensor", tr8, reads=[xn_.R, ident.R], writes=[tp_.R])
                    P.op("vector", lambda e, tp_=tp_, t8=t8: e.tensor_tensor(out=u2T[:, :, t8 * 128:(t8 + 1) * 128], in0=tp_.t[:],
                                                                             in1=gt.t[:, 8:16].unsqueeze(2).to_broadcast([128, 8, 128]), op=ALU.mult),
                         reads=[gt.R], writes=[u2R[t8], tp_.R])
                for fb in range(22):
                    buf = wff[nxt("w", len(wff))]
                    stream(w_gu, fb * 128, 128, 0, buf)
                    stream(w_gu, DFF + fb * 128, 128, 128, buf)
                    if hf == 0:
                        P.op("gpsimd", lambda e, kt=fb: e.dma_start(out=wdn[:, kt, :], in_=w_dn[kt * 128:(kt + 1) * 128, :]),
                             writes=[wdnR] + [w_.R for w_ in wbx], dma=1)
                    for f2 in range(1):
                        f_ = fb
                        for c2 in range(2):
                            sl = slice(c2 * 512, (c2 + 1) * 512)
                            rr = u2R[4 * c2:4 * c2 + 4]
                            pg = proj(buf, 0, u2T, rr, sl)
                            pu = proj(buf, 128, u2T, rr, sl)
                            s_ = sg[nxt("s", 2)]
                            P.op("scalar", lambda e, pg=pg, s_=s_: e.activation(out=s_.t[:], in_=pg.t[:], func=AF.Silu), writes=[s_.R, pg.R])
                            P.op("vector", lambda e, pu=pu, s_=s_, f_=f_, sl=sl: e.tensor_tensor(out=aT(f_)[:, sl], in0=pu.t[:], in1=s_.t[:], op=ALU.mult),
                                 reads=[s_.R], writes=[aR[f_], pu.R])
                for t8 in range(8):
                    tt = hf * 8 + t8
                    for nb2 in range(2):
                        p = pp[nxt("p", NPS)]
                        def f(e, p=p, t8=t8, nb2=nb2):
                            ins = None
                            for kt in range(22):
                                ins = e.matmul(p.t[:], lhsT=aT(kt)[:, t8 * 128:(t8 + 1) * 128], rhs=wdn[:, kt, nb2 * 512:(nb2 + 1) * 512],
                                               start=(kt == 0), stop=(kt == 21))
                            return ins
                        P.op("tensor", f, reads=[wdnR] + aR, writes=[p.R])
                        P.op("vector", lambda e, p=p, tt=tt, nb2=nb2: e.tensor_tensor(out=hview[:, tt, nb2 * 512:(nb2 + 1) * 512], in0=p.t[:],
                                                                                      in1=hview[:, tt, nb2 * 512:(nb2 + 1) * 512], op=ALU.add),
                             writes=[hR[tt], p.R])
                    s2 = rms(tt, hR[tt])
                    o_ = ot[tt % 2]
                    P.op("vector", lambda e, tt=tt, s2=s2, o_=o_: e.scalar_tensor_tensor(out=o_.t[:], in0=hview[:, tt, :], scalar=s2.t[:, 1:2], in1=gf.t[:],
                                                                                         op0=ALU.mult, op1=ALU.mult),
                         reads=[hR[tt], s2.R, gf.R], writes=[o_.R])
                    Ro_ = R("out%d" % tt)
                    P.op("sync", lambda e, o_=o_, tt=tt: e.dma_start(out=out_d[tt * 128:(tt + 1) * 128, :], in_=o_.t[:]), reads=[o_.R], writes=[Ro_], dma=1,
                         semkey=("d", "out"))
                    outs.append(Ro_.w)
                if hf == 0:
                    pass
        for kk in [("d", "ktscr", q_) for q_ in range(4)] + [("d", "vscr", 0), ("d", "vscr", 1)]:
            if kk in P.count:
                outs.append((kk, P.count[kk], "sync", True))
        P.wait_all("sync", outs)
        P.emit()
    return nc


def slot_groups(j):
    order = []
    for i in range(4):
        row = [4 * i + r for r in range(4)]
        order += [g for g in row if g != 4 * i + j] + [4 * i + j]
    return order


def rope_table(positions):
    rd = DH // 4
    inv = (ROPE_THETA ** (-np.arange(0, rd, 2, dtype=np.float32) / rd)).astype(np.float32)
    ang = positions.astype(np.float32)[:, None] * inv[None, :]
    cos, sin = np.cos(ang).astype(np.float32), np.sin(ang).astype(np.float32)
    return np.concatenate([cos, cos, -sin, sin], axis=1)


def core_inputs(c, x, g_mix, w_in, conv_w, g_ffn, g_final=None, shared=None):
    b, j = c // 4, c % 4
    order = slot_groups(j)
    tok = np.concatenate([np.arange(g * 512, (g + 1) * 512) for g in order])
    xbs = np.ascontiguousarray(x[b][tok])
    tab = rope_table(tok)
    tabK = np.ascontiguousarray(tab.reshape(64, 128, 64).transpose(1, 0, 2))
    gtab = np.zeros((128, 40), np.float32)
    gtab[:, 0:8] = g_mix[0].reshape(8, 128).T
    gtab[:, 8:16] = g_ffn[0].reshape(8, 128).T
    for tap in range(3):
        gtab[:, 16 + 8 * tap:24 + 8 * tap] = conv_w[0][tap].reshape(8, 128).T
    xh = np.zeros((8, D), np.float32)
    for i in range(4):
        g0 = (4 * i + j) * 512
        if g0 >= 2:
            xh[2 * i:2 * i + 2] = x[b][g0 - 2:g0]
    mt = np.zeros((3, 16, 32), np.float32)
    for i in range(4):
        for qt in range(4):
            own_blk = 8 * i + 6 + qt // 2
            my_blk = 2 * (4 * i + j) + qt // 2
            for n in range(32):
                actual_blk = 2 * order[n // 2] + n % 2
                past = actual_blk < my_blk
                is_own = (n == own_blk)
                mt[0, 4 * i + qt, n] = 0.0 if past else NEG
                mt[1, 4 * i + qt, n] = 1.0 if past else 0.0
                mt[2, 4 * i + qt, n] = 0.0 if (past or is_own) else -BIG
    mtab = np.ascontiguousarray(np.broadcast_to(mt[None], (128, 3, 16, 32))).astype(np.float32)
    m = {"xb": xbs, "tabK": tabK, "gtab": gtab, "mtab": mtab, "xh": xh}
    if shared is not None:
        m.update(shared)
    else:
        m.update(shared_inputs(w_in, g_final))
    return m, order


def shared_inputs(w_in, g_final=None, w_attn_branch=None, w_conv_branch=None, w_out=None, w_gate_up=None, w_down=None):
    k = np.arange(128)[:, None]
    q = np.arange(128)[None, :]
    cst = np.zeros((128, 256), np.float32)
    cst[:, 0:128] = np.where(k > q, -BIG, 0.0)
    cst[:, 128:256] = -BIG
    eoh = np.zeros((128, 32, 128), np.float32)
    for n in range(32):
        eoh[n, n, :] = 1.0
    sh = {"wkv": np.ascontiguousarray(w_in[0][:, D:3 * D]), "wq": np.ascontiguousarray(w_in[0][:, 0:D]), "cst": cst, "eoh": eoh}
    if w_out is not None:
        sh.update({
            "wrest": np.ascontiguousarray(w_in[0][:, 3 * D:8 * D]),
            "w_ab": np.ascontiguousarray(w_attn_branch[0]), "w_cb": np.ascontiguousarray(w_conv_branch[0]),
            "w_o": np.ascontiguousarray(w_out[0]), "w_gu": np.ascontiguousarray(w_gate_up[0]), "w_dn": np.ascontiguousarray(w_down[0]),
            "gfin": np.ascontiguousarray(np.broadcast_to(np.asarray(g_final, np.float32)[None, :], (128, D))),
        })
    return sh


_NC_CACHE = {}


def kernel(x, g_mix, w_in, conv_w, w_attn_branch, w_conv_branch, w_out, g_ffn, w_gate_up, w_down, g_final):
    args = [np.asarray(a, dtype=np.float32) for a in (x, g_mix, w_in, conv_w, w_attn_branch, w_conv_branch, w_out, g_ffn,
                                                       w_gate_up, w_down, g_final)]
    x, g_mix, w_in, conv_w, w_attn_branch, w_conv_branch, w_out, g_ffn, w_gate_up, w_down, g_final = args
    if "nc" not in _NC_CACHE:
        _NC_CACHE["nc"] = build("full")
    nc = _NC_CACHE["nc"]
    sh = shared_inputs(w_in, g_final, w_attn_branch, w_conv_branch, w_out, w_gate_up, w_down)
    maps, orders = [], []
    for c in range(8):
        m, o = core_inputs(c, x, g_mix, w_in, conv_w, g_ffn, g_final, shared=sh)
        maps.append(m)
        orders.append(o)
    res = run_bass_kernel_spmd(nc, maps, core_ids=list(range(8)))
    out = np.zeros((2, S, D), np.float32)
    for c in range(8):
        b, j = c // 4, c % 4
        o = np.asarray(res.results[c]["out"], dtype=np.float32)
        for i in range(4):
            g = 4 * i + j
            out[b, g * 512:(g + 1) * 512] = o[i * 512:(i + 1) * 512]
    return out
```
